# Optimizing a Trainium2 kernel written in Bass

```python
import math
import jax, jax.numpy as jnp
from jax import lax
import numpy as np

D_MODEL = 2048
BATCH = 2
SEQ = 8192
DEPTH = 4

GRID_W = 64
CTX_LEN = 256
N_MIXERS = 3
N_GLA = (DEPTH + 2) // 3
N_CONV = (DEPTH + 1) // 3
N_DIFF = DEPTH // 3
N_MOD = 6
MLP_HIDDEN = 4 * D_MODEL
EPS = 1e-6
GLA_HEADS = 4
GLA_DK = D_MODEL // 2 // GLA_HEADS
GLA_DV = D_MODEL // GLA_HEADS
GLA_HK = GLA_HEADS * GLA_DK
GLA_HV = GLA_HEADS * GLA_DV
GLA_RANK = 16
GLA_GATE_NORM = 16.0
GLA_CHUNK = 64
CONV_WIDTH = 31
DIFF_HEADS = D_MODEL // 256
DIFF_DK = 128
DIFF_DV = 2 * DIFF_DK
Q_BLOCK = 128
ROPE_BASE = 10000.0
ROPE_AXIS_DIM = DIFF_DK // 2

kernel_name = "hybrid_gla_conformer_diffattn_dit_trunk"

F32 = jnp.float32


def rmsnorm(x, g):
    xf = x.astype(F32)
    y = xf * lax.rsqrt(jnp.mean(xf * xf, axis=-1, keepdims=True) + EPS)
    return (y * g.astype(F32)).astype(x.dtype)


def layernorm(x, g, b):
    xf = x.astype(F32)
    mu = jnp.mean(xf, axis=-1, keepdims=True)
    xc = xf - mu
    y = xc * lax.rsqrt(jnp.mean(xc * xc, axis=-1, keepdims=True) + EPS)
    return (y * g.astype(F32) + b.astype(F32)).astype(x.dtype)


def modulate(h, shift, scale):
    return h * (1.0 + scale) + shift


def squared_relu_mlp(u, w1, w2):
    return jnp.square(jax.nn.relu(u @ w1)) @ w2


def gla_chunk_scan(q, k, v, log_a, s0):
    B, T, H, DK = q.shape
    DV = v.shape[-1]
    N = T // GLA_CHUNK
    dt = q.dtype

    def to_chunks(t):
        return t.reshape(B, N, GLA_CHUNK, H, t.shape[-1]).transpose(1, 0, 3, 2, 4)

    qc, kc, vc, gc = (to_chunks(t) for t in (q, k, v, log_a))
    b = jnp.cumsum(gc.astype(F32), axis=3)
    b_last = b[:, :, :, -1:, :]
    q_dec = (qc.astype(F32) * jnp.exp(b)).astype(dt)
    k_inv = (kc.astype(F32) * jnp.exp(-b)).astype(dt)
    k_end = (kc.astype(F32) * jnp.exp(b_last - b)).astype(dt)
    chunk_decay = jnp.exp(b_last[:, :, :, 0, :]).astype(dt)
    mask = jnp.tril(jnp.ones((GLA_CHUNK, GLA_CHUNK), dtype=bool))
    att = jnp.where(mask, jnp.einsum('nbhcd,nbhsd->nbhcs', q_dec, k_inv), 0)
    o_intra = jnp.einsum('nbhcs,nbhse->nbhce', att, vc)

    def step(state, inp):
        q_n, k_n, v_n, dec_n = inp
        o_n = jnp.einsum('bhcd,bhde->bhce', q_n, state)
        state = dec_n[..., None] * state + jnp.einsum('bhsd,bhse->bhde', k_n, v_n)
        return state, o_n

    s_final, o_inter = lax.scan(step, s0, (q_dec, k_end, vc, chunk_decay))
    o = (o_intra + o_inter).transpose(1, 0, 3, 2, 4).reshape(B, T, H, DV)
    return o, s_final


def gla_mixer(u_lat, u_ctx, w_in, wa1_f, wa2_f, ba_f, wa1_b, wa2_b, ba_b,
              norm_g, w_o, with_ctx_out):
    def project(u):
        B, T, _ = u.shape
        q, k, v, g = jnp.split(u @ w_in, [GLA_HK, 2 * GLA_HK, 2 * GLA_HK + GLA_HV], axis=-1)

        def heads(t, d):
            return t.reshape(B, T, GLA_HEADS, d)

        def log_decay(wa1, wa2, ba):
            z = ((u @ wa1) @ wa2 + ba).astype(F32)
            return heads(jax.nn.log_sigmoid(z) / GLA_GATE_NORM, GLA_DK)

        return (heads(q, GLA_DK) * GLA_DK ** -0.5, heads(k, GLA_DK), heads(v, GLA_DV), g,
                log_decay(wa1_f, wa2_f, ba_f), log_decay(wa1_b, wa2_b, ba_b))

    q_l, k_l, v_l, g_l, af_l, ab_l = project(u_lat)
    q_c, k_c, v_c, g_c, af_c, ab_c = project(u_ctx)
    B = u_lat.shape[0]
    s0 = jnp.zeros((B, GLA_HEADS, GLA_DK, GLA_DV), u_lat.dtype)

    def flip(t):
        return jnp.flip(t, axis=1)

    o_cf, s_cf = gla_chunk_scan(q_c, k_c, v_c, af_c, s0)
    o_lf, _ = gla_chunk_scan(q_l, k_l, v_l, af_l, s_cf)
    o_cb, s_cb = gla_chunk_scan(flip(q_c), flip(k_c), flip(v_c), flip(ab_c), s0)
    o_lb, _ = gla_chunk_scan(flip(q_l), flip(k_l), flip(v_l), flip(ab_l), s_cb)

    def out(o, g):
        B, T = o.shape[:2]
        o = rmsnorm(o, norm_g).reshape(B, T, GLA_HV)
        return (o * jax.nn.silu(g)) @ w_o

    y_lat = out(o_lf + flip(o_lb), g_l)
    y_ctx = out(o_cf + flip(o_cb), g_c) if with_ctx_out else None
    return y_lat, y_ctx


def conformer_conv_mixer(u, w1, b1, dw, dwb, ln_g, ln_b, w2, b2):
    a = u @ w1 + b1
    glu = a[..., :D_MODEL] * jax.nn.sigmoid(a[..., D_MODEL:])
    pad = CONV_WIDTH // 2
    dconv = lax.conv_general_dilated(
        glu, dw[:, None, :].astype(glu.dtype), window_strides=(1,),
        padding=((pad, pad),), dimension_numbers=('NWC', 'WIO', 'NWC'),
        feature_group_count=D_MODEL) + dwb
    return jax.nn.silu(layernorm(dconv, ln_g, ln_b)) @ w2 + b2


def axial_rope_tables(rows):
    pos_row = jnp.repeat(jnp.arange(rows, dtype=F32), GRID_W)
    pos_col = jnp.tile(jnp.arange(GRID_W, dtype=F32), rows)
    inv_freq = ROPE_BASE ** (-jnp.arange(0, ROPE_AXIS_DIM, 2, dtype=F32) / ROPE_AXIS_DIM)
    ang_r = pos_row[:, None] * inv_freq
    ang_c = pos_col[:, None] * inv_freq
    ang = jnp.concatenate([ang_r, ang_r, ang_c, ang_c], axis=-1)
    return jnp.cos(ang), jnp.sin(ang)


def apply_axial_rope(x, cos, sin):
    xf = x.astype(F32)
    xr = xf.reshape(*x.shape[:-1], 2, 2, ROPE_AXIS_DIM // 2)
    rot = jnp.concatenate([-xr[..., 1:, :], xr[..., :1, :]], axis=-2).reshape(x.shape)
    c = cos[None, :, None, None, :]
    s = sin[None, :, None, None, :]
    return (xf * c + rot * s).astype(x.dtype)


def diff_attention_mixer(u_lat, u_ctx, w_qkv, lq1, lk1, lq2, lk2, subln_g, w_o,
                         lambda_init, cos, sin, with_ctx_out):
    def project(u):
        B, T, _ = u.shape
        q, k, v = jnp.split(u @ w_qkv, 3, axis=-1)
        return (q.reshape(B, T, DIFF_HEADS, 2, DIFF_DK),
                k.reshape(B, T, DIFF_HEADS, 2, DIFF_DK),
                v.reshape(B, T, DIFF_HEADS, DIFF_DV))

    q_l, k_l, v_l = project(u_lat)
    q_l = apply_axial_rope(q_l, cos, sin)
    k_l = apply_axial_rope(k_l, cos, sin)
    q_c, k_c, v_c = project(u_ctx)
    lam = (jnp.exp(jnp.sum(lq1.astype(F32) * lk1.astype(F32)))
           - jnp.exp(jnp.sum(lq2.astype(F32) * lk2.astype(F32))) + lambda_init)

    def attend(q, keys, vals):
        s = jnp.einsum('bqhjd,bkhjd->bhjqk', q, keys,
                       preferred_element_type=F32) * DIFF_DK ** -0.5
        p = jax.nn.softmax(s, axis=-1)
        a = (p[:, :, 0] - lam * p[:, :, 1]).astype(vals.dtype)
        return jnp.einsum('bhqk,bkhe->bqhe', a, vals)

    k_all = jnp.concatenate([k_c, k_l], axis=1)
    v_all = jnp.concatenate([v_c, v_l], axis=1)
    B, T = u_lat.shape[:2]
    nb = T // Q_BLOCK
    q_blocks = q_l.reshape(B, nb, Q_BLOCK, DIFF_HEADS, 2, DIFF_DK).transpose(1, 0, 2, 3, 4, 5)
    o_l = lax.map(lambda qb: attend(qb, k_all, v_all), q_blocks)
    o_l = o_l.transpose(1, 0, 2, 3, 4).reshape(B, T, DIFF_HEADS, DIFF_DV)

    def out(o):
        B, T = o.shape[:2]
        o = rmsnorm(o, subln_g) * (1.0 - lambda_init)
        return o.reshape(B, T, DIFF_HEADS * DIFF_DV) @ w_o

    y_lat = out(o_l)
    y_ctx = out(attend(q_c, k_c, v_c)) if with_ctx_out else None
    return y_lat, y_ctx


def setup_inputs(seed: int = 0) -> dict:
    key = jax.random.key(seed)
    ks = iter(jax.random.split(key, 40))
    D = D_MODEL

    def nrm(shape, scale):
        return jax.random.normal(next(ks), shape, F32) * scale

    def gain(shape):
        return 1.0 + nrm(shape, 0.02)

    def bias(shape):
        return nrm(shape, 0.02)

    return {
        "x": nrm((BATCH, SEQ, D), 1.0),
        "c": nrm((BATCH, D), 1.0),
        "ctx": nrm((BATCH, CTX_LEN, D), 1.0),
        "c_ctx": nrm((D,), 1.0),
        "mod_w": nrm((DEPTH, D, N_MOD * D), D ** -0.5),
        "mod_b": bias((DEPTH, N_MOD * D)),
        "norm_mix_g": gain((DEPTH, D)),
        "norm_mlp_g": gain((DEPTH, D)),
        "mlp_w1": nrm((DEPTH, D, MLP_HIDDEN), D ** -0.5),
        "mlp_w2": nrm((DEPTH, MLP_HIDDEN, D), MLP_HIDDEN ** -0.5),
        "gla_w_in": nrm((N_GLA, D, 2 * GLA_HK + 2 * GLA_HV), D ** -0.5),
        "gla_wa1_f": nrm((N_GLA, D, GLA_RANK), D ** -0.5),
        "gla_wa2_f": nrm((N_GLA, GLA_RANK, GLA_HK), GLA_RANK ** -0.5),
        "gla_ba_f": bias((N_GLA, GLA_HK)),
        "gla_wa1_b": nrm((N_GLA, D, GLA_RANK), D ** -0.5),
        "gla_wa2_b": nrm((N_GLA, GLA_RANK, GLA_HK), GLA_RANK ** -0.5),
        "gla_ba_b": bias((N_GLA, GLA_HK)),
        "gla_norm_g": gain((N_GLA, GLA_DV)),
        "gla_w_o": nrm((N_GLA, GLA_HV, D), GLA_HV ** -0.5),
        "conv_w1": nrm((N_CONV, D, 2 * D), D ** -0.5),
        "conv_b1": bias((N_CONV, 2 * D)),
        "conv_dw": nrm((N_CONV, CONV_WIDTH, D), CONV_WIDTH ** -0.5),
        "conv_dwb": bias((N_CONV, D)),
        "conv_ln_g": gain((N_CONV, D)),
        "conv_ln_b": bias((N_CONV, D)),
        "conv_w2": nrm((N_CONV, D, D), D ** -0.5),
        "conv_b2": bias((N_CONV, D)),
        "diff_w_qkv": nrm((N_DIFF, D, 3 * DIFF_HEADS * 2 * DIFF_DK), D ** -0.5),
        "diff_lq1": nrm((N_DIFF, DIFF_DK), 0.1),
        "diff_lk1": nrm((N_DIFF, DIFF_DK), 0.1),
        "diff_lq2": nrm((N_DIFF, DIFF_DK), 0.1),
        "diff_lk2": nrm((N_DIFF, DIFF_DK), 0.1),
        "diff_subln_g": gain((N_DIFF, DIFF_DV)),
        "diff_w_o": nrm((N_DIFF, DIFF_HEADS * DIFF_DV, D), (DIFF_HEADS * DIFF_DV) ** -0.5),
        "final_g": gain((D,)),
    }


def reference(x, c, ctx, c_ctx, mod_w, mod_b, norm_mix_g, norm_mlp_g, mlp_w1, mlp_w2,
              gla_w_in, gla_wa1_f, gla_wa2_f, gla_ba_f, gla_wa1_b, gla_wa2_b, gla_ba_b,
              gla_norm_g, gla_w_o,
              conv_w1, conv_b1, conv_dw, conv_dwb, conv_ln_g, conv_ln_b, conv_w2, conv_b2,
              diff_w_qkv, diff_lq1, diff_lk1, diff_lq2, diff_lk2, diff_subln_g, diff_w_o,
              final_g):
    n_tokens = x.shape[1]
    rows = n_tokens // GRID_W
    cos, sin = axial_rope_tables(rows)
    sc = jax.nn.silu(c)
    scc = jax.nn.silu(c_ctx)
    h_lat, h_ctx = x, ctx
    for i in range(DEPTH):
        with_ctx = i < DEPTH - 1
        mod_lat = (sc @ mod_w[i] + mod_b[i])[:, None, :]
        mod_ctx = (scc @ mod_w[i] + mod_b[i])[None, None, :]
        sh1, s1, g1, sh2, s2, g2 = jnp.split(mod_lat, N_MOD, axis=-1)
        csh1, cs1, cg1, csh2, cs2, cg2 = jnp.split(mod_ctx, N_MOD, axis=-1)

        u_lat = modulate(rmsnorm(h_lat, norm_mix_g[i]), sh1, s1)
        u_ctx = modulate(rmsnorm(h_ctx, norm_mix_g[i]), csh1, cs1)
        kind, j = i % N_MIXERS, i // N_MIXERS
        if kind == 0:
            y_lat, y_ctx = gla_mixer(u_lat, u_ctx, gla_w_in[j], gla_wa1_f[j], gla_wa2_f[j],
                                     gla_ba_f[j], gla_wa1_b[j], gla_wa2_b[j], gla_ba_b[j],
                                     gla_norm_g[j], gla_w_o[j], with_ctx)
        elif kind == 1:
            conv_args = (conv_w1[j], conv_b1[j], conv_dw[j], conv_dwb[j], conv_ln_g[j],
                         conv_ln_b[j], conv_w2[j], conv_b2[j])
            y_lat = conformer_conv_mixer(u_lat, *conv_args)
            y_ctx = conformer_conv_mixer(u_ctx, *conv_args) if with_ctx else None
        else:
            lambda_init = 0.8 - 0.6 * math.exp(-0.3 * i)
            y_lat, y_ctx = diff_attention_mixer(u_lat, u_ctx, diff_w_qkv[j], diff_lq1[j],
                                                diff_lk1[j], diff_lq2[j], diff_lk2[j],
                                                diff_subln_g[j], diff_w_o[j], lambda_init,
                                                cos, sin, with_ctx)
        h_lat = h_lat + g1 * y_lat
        h_lat = h_lat + g2 * squared_relu_mlp(
            modulate(rmsnorm(h_lat, norm_mlp_g[i]), sh2, s2), mlp_w1[i], mlp_w2[i])
        if with_ctx:
            h_ctx = h_ctx + cg1 * y_ctx
            h_ctx = h_ctx + cg2 * squared_relu_mlp(
                modulate(rmsnorm(h_ctx, norm_mlp_g[i]), csh2, cs2), mlp_w1[i], mlp_w2[i])
    return rmsnorm(h_lat, final_g)
```

```python
import contextlib
import math
import numpy as np
import concourse.bass as bass
import concourse.mybir as mybir
from concourse.bass_utils import run_bass_kernel_spmd

F32 = mybir.dt.float32
BF16 = mybir.dt.bfloat16
AF = mybir.ActivationFunctionType
ALU = mybir.AluOpType
AX = mybir.AxisListType

D = 2048
KC = 16
NLAT = 2048
NCTX = 256
NTOK = NLAT + NCTX
TT = 1152
BLK = 384
NBLK = 192
HID = 8192
EPS = 1e-6
N_CORES = 8

SEM_CHUNK = 30000
N_DMA_SEMS = 12


class Buf:
    __slots__ = ("name", "w", "r")

    def __init__(self, name=""):
        self.name = name
        self.w = None
        self.r = {}


class Prog:
    ENGS = ("pe", "dve", "act", "pool", "sp")

    def __init__(self, nc):
        self.nc = nc
        self.ops = {e: [] for e in self.ENGS}
        self.dq = {"sp": "sp", "act": "act", "pool": "pool"}
        self.dma_cnt = {q: 0 for q in self.dq}
        self.seen = {e: {} for e in self.ENGS}

    def _deps(self, eng, reads, writes):
        deps = {}

        def add(d):
            if d is None:
                return
            kind, key, idx = d
            if kind == "eng" and key == eng and eng in ("pe", "sp"):
                return
            k = (kind, key)
            if deps.get(k, 0) < idx:
                deps[k] = idx

        for b in reads:
            add(b.w)
        for b in writes:
            add(b.w)
            for d in b.r.values():
                add(d)
        out = []
        for (kind, key), idx in deps.items():
            if self.seen[eng].get((kind, key), 0) >= idx:
                continue
            self.seen[eng][(kind, key)] = idx
            out.append((kind, key, idx))
        return out

    def _mark(self, me, reads, writes):
        for b in reads:
            b.r[(me[0], me[1])] = me
        for b in writes:
            b.w = me
            b.r = {}

    def _push(self, eng, rec):
        self.ops[eng].append(rec)
        for (kind, key, i) in rec["waits"]:
            if kind == "eng":
                self.ops[key][i - 1]["inc"] = True

    def op(self, eng, fn, reads=(), writes=()):
        waits = self._deps(eng, reads, writes)
        idx = len(self.ops[eng]) + 1
        self._push(eng, {"fn": fn, "waits": waits, "inc": False, "dma": None})
        self._mark(("eng", eng, idx), reads, writes)

    def dma(self, queue, out, in_, reads=(), writes=(), **kw):
        eng = self.dq[queue]
        waits = self._deps(eng, reads, writes)
        n = self.dma_cnt[queue]
        slot = n % N_DMA_SEMS
        gen = n // N_DMA_SEMS + 1
        self.dma_cnt[queue] = n + 1
        if gen > 1:
            k = ("dma", (queue, slot))
            if self.seen[eng].get(k, 0) < gen - 1:
                self.seen[eng][k] = gen - 1
                waits.append(("dma", (queue, slot), gen - 1))
        self._push(eng, {"fn": None, "waits": waits, "inc": False,
                         "dma": (queue, slot, out, in_, kw)})
        self._mark(("dma", (queue, slot), gen), reads, writes)

    def barrier(self):
        last = {}
        for e in self.ENGS:
            for i in range(len(self.ops[e]), 0, -1):
                if self.ops[e][i - 1]["fn"] is not None:
                    last[e] = i
                    break
        dl = {}
        for q in self.dq:
            n = self.dma_cnt[q]
            for slot in range(min(n, N_DMA_SEMS)):
                gen = (n - 1 - slot) // N_DMA_SEMS + 1
                dl[(q, slot)] = gen
        for e in self.ENGS:
            waits = []
            for e2, i in last.items():
                if e2 != e and self.seen[e].get(("eng", e2), 0) < i:
                    self.seen[e][("eng", e2)] = i
                    waits.append(("eng", e2, i))
            for k, gen in dl.items():
                if self.seen[e].get(("dma", k), 0) < gen:
                    self.seen[e][("dma", k)] = gen
                    waits.append(("dma", k, gen))
            self._push(e, {"fn": None, "waits": waits, "inc": False, "dma": None})

    def wait_all(self, eng, bufs):
        waits = self._deps(eng, [], bufs)
        self._push(eng, {"fn": None, "waits": waits, "inc": False, "dma": None})

    def emit(self):
        nc = self.nc
        hmap = {"pe": "tensor", "dve": "vector", "act": "scalar", "pool": "gpsimd", "sp": "sync"}
        incval, nsem = {}, {}
        for e in self.ENGS:
            g, vals = 0, []
            for rec in self.ops[e]:
                if rec["inc"]:
                    g += 1
                vals.append(g)
            incval[e] = vals
            nsem[e] = (g + SEM_CHUNK - 1) // SEM_CHUNK
        with contextlib.ExitStack() as st:
            esem = {e: [st.enter_context(nc.semaphore(f"s_{e}{i}")) for i in range(nsem[e])]
                    for e in self.ENGS}
            dsem = {q: [st.enter_context(nc.semaphore(f"d_{q}{i}"))
                        for i in range(min(N_DMA_SEMS, self.dma_cnt[q]))] for q in self.dq}
            block = st.enter_context(nc.Block())

            def run(e, h):
                vals = incval[e]
                for i, rec in enumerate(self.ops[e]):
                    for (kind, key, idx) in rec["waits"]:
                        if kind == "eng":
                            g = incval[key][idx - 1]
                            si = (g - 1) // SEM_CHUNK
                            h.wait_ge(esem[key][si], g - si * SEM_CHUNK)
                        else:
                            q, slot = key
                            h.wait_ge(dsem[q][slot], 16 * idx)
                    if rec["dma"] is not None:
                        q, slot, out, in_, kw = rec["dma"]
                        h.dma_start(out=out, in_=in_, **kw).then_inc(dsem[q][slot], 16)
                    elif rec["fn"] is not None:
                        ins = rec["fn"](h)
                        if rec["inc"]:
                            g = vals[i]
                            si = (g - 1) // SEM_CHUNK
                            ins.then_inc(esem[e][si], 1)

            for e in self.ENGS:
                getattr(block, hmap[e])(lambda h, e=e: run(e, h))


class Builder:
    def __init__(self):
        self.nc = bass.Bass("TRN2", target_bir_lowering=False)
        self.P = Prog(self.nc)
        self.st = contextlib.ExitStack()
        self.nps = 0
        self.nw = 0
        self.io_bufs = []
        nc = self.nc
        self.ps = [self.st.enter_context(nc.psum_tensor(f"ps{i}", [128, 512], F32)) for i in range(8)]
        self.psb = [Buf(f"ps{i}") for i in range(8)]
        self.ones_bf = self.sb("ones_bf", [128, 128], BF16)
        self.b_const = Buf("const")
        self.eps_c = self.sb("eps_c", [128, 2], F32)
        self.P.op("pool", lambda h: h.memset(self.ones_bf[:], 1.0), writes=[self.b_const])
        self.P.op("pool", lambda h: h.memset(self.eps_c[:], EPS), writes=[self.b_const])

    def sb(self, name, shape, dt):
        return self.st.enter_context(self.nc.sbuf_tensor(name, shape, dt))

    def din(self, name, shape, dt=F32):
        return self.nc.dram_tensor(name, list(shape), dt, kind="ExternalInput").ap()

    def dout(self, name, shape, dt=F32):
        ap = self.nc.dram_tensor(name, list(shape), dt, kind="ExternalOutput").ap()
        return ap

    def wpool(self, n=3):
        self.wt = [self.sb(f"wt{i}", [128, 16, 512], BF16) for i in range(n)]
        self.wtb = [Buf(f"wt{i}") for i in range(n)]

    def next_ps(self):
        i = self.nps % 8
        self.nps += 1
        return self.ps[i], self.psb[i]

    def load_w(self, src_ap, view="k16"):
        i = self.nw % len(self.wt)
        self.nw += 1
        t, b = self.wt[i], self.wtb[i]
        if view == "k16":
            dst = t[:]
        else:
            dst = t[:].rearrange("p a (h n) -> p (a h) n", h=2)
        self.P.dma("pool", dst, src_ap, writes=[b])
        return t, b

    def finish(self, out_bufs):
        self.P.wait_all("sp", out_bufs)
        self.P.emit()
        self.st.close()
        return self.nc


def wview(W, r0, nrows, c0, ncols):
    return W[r0:r0 + nrows, c0:c0 + ncols].rearrange("(c p) n -> p c n", p=128)


def tok_blocks(t0, t1, blk=BLK):
    out = []
    t = t0
    while t < t1:
        n = min(blk, t1 - t)
        out.append((t, n))
        t += n
    return out


def emit_norm(B, hT, t0, nt, uT, b_uT, gm_lat, sh_lat, gm_ctx, sh_ctx, b_vec, pools, hdeps=()):
    P = B.P
    hs, hsb, sq, sqb, rs, rsb = pools
    for bi, (t, n) in enumerate(tok_blocks(t0, t0 + nt, NBLK)):
        s = bi % 2
        P.dma("sp", hs[s][:, :, :n], hT[:, t:t + n].rearrange("(c p) n -> p c n", p=128), reads=list(hdeps), writes=[hsb[s]])
        P.op("act", lambda h, s=s, n=n: h.activation(out=sq[s][:, :, :n], in_=hs[s][:, :, :n], func=AF.Square),
             reads=[hsb[s]], writes=[sqb[s]])
        ps, psb = B.next_ps()
        for kc in range(KC):
            P.op("pe", lambda h, ps=ps, s=s, kc=kc, n=n: h.matmul(ps[:, :n], lhsT=B.ones_bf[:], rhs=sq[s][:, kc, :n], start=(kc == 0), stop=(kc == KC - 1)),
                 reads=[sqb[s], B.b_const], writes=[psb])
        P.op("act", lambda h, ps=ps, s=s, n=n: h.activation(out=rs[s][:, :n], in_=ps[:, :n], func=AF.Sqrt, scale=1.0 / D, bias=B.eps_c[:, 0:1]),
             reads=[psb, B.b_const], writes=[rsb[s]])
        P.op("dve", lambda h, s=s, n=n: h.reciprocal(out=rs[s][:, :n], in_=rs[s][:, :n]),
             reads=[rsb[s]], writes=[rsb[s]])
        for kc in range(KC):
            P.op("dve", lambda h, s=s, n=n, kc=kc: h.tensor_tensor(out=hs[s][:, kc, :n], in0=hs[s][:, kc, :n], in1=rs[s][:, :n], op=ALU.mult),
                 reads=[hsb[s], rsb[s]], writes=[hsb[s]])
        segs = []
        lat_end = min(t + n, NLAT)
        if t < lat_end:
            segs.append((0, lat_end - t, gm_lat, sh_lat))
        if t + n > NLAT:
            c0 = max(t, NLAT) - t
            segs.append((c0, n, gm_ctx, sh_ctx))
        o = t - t0
        for kc in range(KC):
            for (a, b_, gm, sh) in segs:
                eng = "dve"
                P.op(eng, lambda h, s=s, kc=kc, a=a, b_=b_, gm=gm, sh=sh, o=o: h.tensor_scalar(
                    out=uT[:, kc, o + a:o + b_], in0=hs[s][:, kc, a:b_], scalar1=gm[:, kc:kc + 1], scalar2=sh[:, kc:kc + 1], op0=ALU.mult, op1=ALU.add),
                    reads=[hsb[s], b_vec], writes=[b_uT])


def norm_pools(B):
    hs = [B.sb(f"hs{i}", [128, KC, NBLK], F32) for i in range(2)]
    sq = [B.sb(f"sq{i}", [128, KC, NBLK], BF16) for i in range(2)]
    rs = [B.sb(f"rs{i}", [128, NBLK], F32) for i in range(2)]
    return (hs, [Buf() for _ in range(2)], sq, [Buf() for _ in range(2)], rs, [Buf() for _ in range(2)])


def emit_modvec(B, vec, b_vec, out, ig, isc, ish):
    P = B.P
    P.op("dve", lambda h: h.scalar_tensor_tensor(out=out[:], in0=vec[:, isc * 16:isc * 16 + 16], scalar=1.0, in1=vec[:, ig * 16:ig * 16 + 16], op0=ALU.add, op1=ALU.mult),
         reads=[b_vec], writes=[b_vec])


def col_segs(t, n, lat, ctx):
    segs = []
    lat_end = min(t + n, NLAT)
    if t < lat_end:
        segs.append((0, lat_end - t, lat))
    if t + n > NLAT:
        segs.append((max(t, NLAT) - t, n, ctx))
    return segs


def emit_proj_residual(B, xT, b_xT, W, bias, g_lat, g_ctx, b_vec, hT_in, hT_out, hrows, t0, nt, rmw):
    P = B.P
    rt, rtb, rl, rlb, tmp, tmpb = rmw
    blks = tok_blocks(0, nt)
    for nb in range(4):
        wt, wb = B.load_w(wview(W, 0, D, nb * 512, 512))
        for m in range(4):
            dm = nb * 4 + m
            s = dm % 2
            P.dma("sp", rt[s][:, :nt], hT_in[dm * 128:(dm + 1) * 128, t0:t0 + nt], reads=[hrows[dm]], writes=[rtb[s]])
            for (o, n) in blks:
                ps, psb = B.next_ps()
                for kc in range(KC):
                    P.op("pe", lambda h, ps=ps, wt=wt, kc=kc, m=m, o=o, n=n: h.matmul(ps[:, :n], lhsT=wt[:, kc, m * 128:(m + 1) * 128], rhs=xT[:, kc, o:o + n], start=(kc == 0), stop=(kc == KC - 1)),
                         reads=[wb, b_xT], writes=[psb])
                for (a, b_, gv) in col_segs(t0 + o, n, g_lat, g_ctx):
                    if bias is None:
                        P.op("dve", lambda h, ps=ps, s=s, dm=dm, o=o, a=a, b_=b_, gv=gv: h.scalar_tensor_tensor(
                            out=rt[s][:, o + a:o + b_], in0=ps[:, a:b_], scalar=gv[:, dm:dm + 1], in1=rt[s][:, o + a:o + b_], op0=ALU.mult, op1=ALU.add),
                            reads=[psb, b_vec, rtb[s]], writes=[rtb[s]])
                    else:
                        P.op("dve", lambda h, ps=ps, dm=dm, a=a, b_=b_, gv=gv: h.tensor_scalar(
                            out=tmp[:, a:b_], in0=ps[:, a:b_], scalar1=bias[:, dm:dm + 1], scalar2=gv[:, dm:dm + 1], op0=ALU.add, op1=ALU.mult),
                            reads=[psb, b_vec], writes=[tmpb])
                        P.op("dve", lambda h, s=s, o=o, a=a, b_=b_: h.tensor_tensor(out=rt[s][:, o + a:o + b_], in0=rt[s][:, o + a:o + b_], in1=tmp[:, a:b_], op=ALU.add),
                             reads=[tmpb, rtb[s]], writes=[rtb[s]])
            P.dma("sp", hT_out[dm * 128:(dm + 1) * 128, t0:t0 + nt], rt[s][:, :nt], reads=[rtb[s]], writes=[hrows[dm]])


def emit_mlp(B, hT, w1, w2, uT, b_uT, hh, b_hh, vecs, b_vec, npools, rmw, hrows):
    P = B.P
    gm_lat, sh_lat, g_lat, gm_ctx, sh_ctx, g_ctx = vecs
    rt, rtb, rl, rlb, tmp, tmpb = rmw
    for ti, (t0, nt) in enumerate([(0, TT), (TT, TT)]):
        emit_norm(B, hT, t0, nt, uT, b_uT, gm_lat, sh_lat, gm_ctx, sh_ctx, b_vec, npools, hdeps=hrows)
        blks = tok_blocks(0, nt)
        for half in range(2):
            for nb in range(8):
                wt, wb = B.load_w(wview(w1, 0, D, half * 4096 + nb * 512, 512))
                for m in range(4):
                    hc = nb * 4 + m
                    for (o, n) in blks:
                        ps, psb = B.next_ps()
                        for kc in range(KC):
                            P.op("pe", lambda h, ps=ps, wt=wt, kc=kc, m=m, o=o, n=n: h.matmul(ps[:, :n], lhsT=wt[:, kc, m * 128:(m + 1) * 128], rhs=uT[:, kc, o:o + n], start=(kc == 0), stop=(kc == KC - 1)),
                                 reads=[wb, b_uT], writes=[psb])
                        s = B.nps % 2
                        P.op("act", lambda h, ps=ps, s=s, n=n: h.activation(out=rl[s][:, :n], in_=ps[:, :n], func=AF.Relu),
                             reads=[psb], writes=[rlb[s]])
                        P.op("dve", lambda h, ps=ps, s=s, hc=hc, o=o, n=n: h.tensor_tensor(out=hh[:, hc, o:o + n], in0=ps[:, :n], in1=rl[s][:, :n], op=ALU.mult),
                             reads=[psb, rlb[s]], writes=[b_hh])
            for cb in range(8):
                wt, wb = B.load_w(wview(w2, half * 4096, 4096, cb * 256, 256), view="k32")
                wv = wt[:].rearrange("p a (h n) -> p (a h) n", h=2)
                for m in range(2):
                    dm = cb * 2 + m
                    s = dm % 2
                    P.dma("sp", rt[s][:, :nt], hT[dm * 128:(dm + 1) * 128, t0:t0 + nt], reads=[hrows[dm]], writes=[rtb[s]])
                    for (o, n) in blks:
                        ps, psb = B.next_ps()
                        for hc in range(32):
                            P.op("pe", lambda h, ps=ps, wv=wv, hc=hc, m=m, o=o, n=n: h.matmul(ps[:, :n], lhsT=wv[:, hc, m * 128:(m + 1) * 128], rhs=hh[:, hc, o:o + n], start=(hc == 0), stop=(hc == 31)),
                                 reads=[wb, b_hh], writes=[psb])
                        for (a, b_, gv) in col_segs(t0 + o, n, g_lat, g_ctx):
                            P.op("dve", lambda h, ps=ps, s=s, dm=dm, o=o, a=a, b_=b_, gv=gv: h.scalar_tensor_tensor(
                                out=rt[s][:, o + a:o + b_], in0=ps[:, a:b_], scalar=gv[:, dm:dm + 1], in1=rt[s][:, o + a:o + b_], op0=ALU.mult, op1=ALU.add),
                                reads=[psb, b_vec, rtb[s]], writes=[rtb[s]])
                    P.dma("sp", hT[dm * 128:(dm + 1) * 128, t0:t0 + nt], rt[s][:, :nt], reads=[rtb[s]], writes=[hrows[dm]])


V_NMIX, V_NMLP, V_S1L, V_SH1L, V_G1L, V_S2L, V_SH2L, V_G2L, V_S1C, V_SH1C, V_G1C, V_S2C, V_SH2C, V_G2C, V_X0, V_X1, V_X2, V_X3 = range(18)
NVEC = 18


def vcol(vec, i):
    return vec[:, i * 16:(i + 1) * 16]


class Common:
    def __init__(self, B, need_mlp):
        self.B = B
        P = B.P
        self.vec_d = B.din("vec", [128, NVEC * 16])
        self.vec = B.sb("vec_s", [128, NVEC * 16], F32)
        self.b_vec = Buf("vec")
        P.dma("sp", self.vec[:], self.vec_d[:, :], writes=[self.b_vec])
        self.gm = {}
        for name, ig, isc in (("mixl", V_NMIX, V_S1L), ("mixc", V_NMIX, V_S1C), ("mlpl", V_NMLP, V_S2L), ("mlpc", V_NMLP, V_S2C)):
            t = B.sb("gm_" + name, [128, 16], F32)
            emit_modvec(B, self.vec, self.b_vec, t, ig, isc, 0)
            self.gm[name] = t
        B.wpool(2)
        self.uT = B.sb("uT", [128, KC, TT], BF16)
        self.b_uT = Buf("uT")
        self.npools = norm_pools(B)
        rt = [B.sb(f"rt{i}", [128, TT], F32) for i in range(2)]
        rl = [B.sb(f"rl{i}", [128, BLK], BF16) for i in range(2)]
        tmp = B.sb("ptmp", [128, BLK], F32)
        self.rmw = (rt, [Buf(), Buf()], rl, [Buf(), Buf()], tmp, Buf())
        self.hrows = [Buf(f"hrow{i}") for i in range(KC)]

    def v(self, i):
        return vcol(self.vec, i)

    def norm_mix(self, hT, t0, nt, hdeps=()):
        emit_norm(self.B, hT, t0, nt, self.uT, self.b_uT, self.gm["mixl"], self.v(V_SH1L), self.gm["mixc"], self.v(V_SH1C), self.b_vec, self.npools, hdeps=hdeps)

    def mlp(self, hT, w1, w2):
        B = self.B
        B.P.barrier()
        hh = B.sb("hh", [128, 32, TT], BF16)
        vecs = (self.gm["mlpl"], self.v(V_SH2L), self.v(V_G2L), self.gm["mlpc"], self.v(V_SH2C), self.v(V_G2C))
        emit_mlp(B, hT, w1, w2, self.uT, self.b_uT, hh, Buf("hh"), vecs, self.b_vec, self.npools, self.rmw, self.hrows)


def emit_final_norm(C, hT, outT, final_g):
    B, P = C.B, C.B.P
    hs, hsb, sq, sqb, rs, rsb = C.npools
    b_o = Buf("outT")
    for bi, (t, n) in enumerate(tok_blocks(0, NLAT, NBLK)):
        s = bi % 2
        P.dma("sp", hs[s][:, :, :n], hT[:, t:t + n].rearrange("(c p) n -> p c n", p=128), reads=C.hrows, writes=[hsb[s]])
        P.op("act", lambda h, s=s, n=n: h.activation(out=sq[s][:, :, :n], in_=hs[s][:, :, :n], func=AF.Square), reads=[hsb[s]], writes=[sqb[s]])
        ps, psb = B.next_ps()
        for kc in range(KC):
            P.op("pe", lambda h, ps=ps, s=s, kc=kc, n=n: h.matmul(ps[:, :n], lhsT=B.ones_bf[:], rhs=sq[s][:, kc, :n], start=(kc == 0), stop=(kc == KC - 1)),
                 reads=[sqb[s], B.b_const], writes=[psb])
        P.op("act", lambda h, ps=ps, s=s, n=n: h.activation(out=rs[s][:, :n], in_=ps[:, :n], func=AF.Sqrt, scale=1.0 / D, bias=B.eps_c[:, 0:1]), reads=[psb, B.b_const], writes=[rsb[s]])
        P.op("dve", lambda h, s=s, n=n: h.reciprocal(out=rs[s][:, :n], in_=rs[s][:, :n]), reads=[rsb[s]], writes=[rsb[s]])
        for kc in range(KC):
            P.op("dve", lambda h, s=s, n=n, kc=kc: h.scalar_tensor_tensor(out=hs[s][:, kc, :n], in0=hs[s][:, kc, :n], scalar=final_g[:, kc:kc + 1], in1=rs[s][:, :n], op0=ALU.mult, op1=ALU.mult),
                 reads=[hsb[s], rsb[s], C.b_vec], writes=[hsb[s]])
        P.dma("sp", outT[:, t:t + n].rearrange("(c p) n -> p c n", p=128), hs[s][:, :, :n], reads=[hsb[s]], writes=[b_o])
    return [b_o]


def build_mod():
    B = Builder()
    P = B.P
    cT = B.din("cT", [128, KC * 3])
    mw = B.din("mw", [D, 6144])
    mb = B.din("mb", [128, 48])
    out = B.dout("modT", [128, 48 * 3])
    B.wpool(3)
    cs = B.sb("cs", [128, KC * 3], F32)
    sc = B.sb("sc", [128, KC, 3], BF16)
    mbs = B.sb("mbs", [128, 48], F32)
    res = B.sb("res", [128, 48, 3], F32)
    b_c, b_sc, b_mb, b_res, b_o = Buf(), Buf(), Buf(), Buf(), Buf()
    P.dma("sp", cs[:], cT[:, :], writes=[b_c])
    P.dma("sp", mbs[:], mb[:, :], writes=[b_mb])
    P.op("act", lambda h: h.activation(out=sc[:].rearrange("p a b -> p (a b)"), in_=cs[:], func=AF.Silu), reads=[b_c], writes=[b_sc])
    for nb in range(12):
        wt, wb = B.load_w(wview(mw, 0, D, nb * 512, 512))
        for m in range(4):
            ch = nb * 4 + m
            ps, psb = B.next_ps()
            for kc in range(KC):
                P.op("pe", lambda h, ps=ps, wt=wt, kc=kc, m=m: h.matmul(ps[:, :3], lhsT=wt[:, kc, m * 128:(m + 1) * 128], rhs=sc[:, kc, :], start=(kc == 0), stop=(kc == KC - 1)),
                     reads=[wb, b_sc], writes=[psb])
            P.op("dve", lambda h, ps=ps, ch=ch: h.tensor_scalar(out=res[:, ch, :], in0=ps[:, :3], scalar1=mbs[:, ch:ch + 1], scalar2=None, op0=ALU.add),
                 reads=[psb, b_mb], writes=[b_res])
    P.dma("sp", out[:, :], res[:].rearrange("p a b -> p (a b)"), reads=[b_res], writes=[b_o])
    return B.finish([b_o])


HALO = 15
LATP = NLAT + 2 * HALO
CTXP = NCTX + 2 * HALO


def build_conv_a():
    B = Builder()
    P = B.P
    hT = B.din("hT", [D, NTOK])
    w1 = B.din("w1", [D, 2 * D])
    C = Common(B, False)
    gl = B.dout("gluT", [D, NTOK])
    sg = [B.sb(f"sg{i}", [128, BLK], F32) for i in range(2)]
    sgb = [Buf(), Buf()]
    go = [B.sb(f"go{i}", [128, TT], F32) for i in range(2)]
    gob = [Buf(), Buf()]
    b_out = Buf("glu")
    b1a, b1g = C.v(V_X0), C.v(V_X1)
    it = 0
    for (t0, nt) in [(0, TT), (TT, TT)]:
        C.norm_mix(hT, t0, nt)
        blks = tok_blocks(0, nt)
        for j in range(4):
            wa, wab = B.load_w(wview(w1, 0, D, j * 512, 512))
            wg, wgb = B.load_w(wview(w1, 0, D, D + j * 512, 512))
            for m in range(4):
                c = j * 4 + m
                s = c % 2
                for (o, n) in blks:
                    pa, pab = B.next_ps()
                    pg, pgb = B.next_ps()
                    for (ps, psb, wt, wb) in ((pa, pab, wa, wab), (pg, pgb, wg, wgb)):
                        for kc in range(KC):
                            P.op("pe", lambda h, ps=ps, wt=wt, kc=kc, m=m, o=o, n=n: h.matmul(ps[:, :n], lhsT=wt[:, kc, m * 128:(m + 1) * 128], rhs=C.uT[:, kc, o:o + n], start=(kc == 0), stop=(kc == KC - 1)),
                                 reads=[wb, C.b_uT], writes=[psb])
                    q = it % 2
                    it += 1
                    P.op("act", lambda h, pg=pg, q=q, n=n, c=c: h.activation(out=sg[q][:, :n], in_=pg[:, :n], func=AF.Sigmoid, bias=b1g[:, c:c + 1]),
                         reads=[pgb, C.b_vec], writes=[sgb[q]])
                    P.op("dve", lambda h, pa=pa, q=q, s=s, o=o, n=n, c=c: h.scalar_tensor_tensor(out=go[s][:, o:o + n], in0=pa[:, :n], scalar=b1a[:, c:c + 1], in1=sg[q][:, :n], op0=ALU.add, op1=ALU.mult),
                         reads=[pab, sgb[q], C.b_vec], writes=[gob[s]])
                P.dma("sp", gl[c * 128:(c + 1) * 128, t0:t0 + nt], go[s][:, :nt], reads=[gob[s]], writes=[b_out])
    return B.finish([b_out])


def build_conv_b(with_final=False):
    B = Builder()
    P = B.P
    hT = B.din("hT", [D, NTOK])
    gp = B.din("gp", [D, LATP + CTXP])
    dwv = B.din("dw", [128, KC * 32])
    w2c = B.din("cw2", [D, D])
    w1 = B.din("w1", [D, HID])
    w2 = B.din("w2", [HID, D])
    hT_out = B.dout("hT_out", [D, NTOK])
    C = Common(B, True)
    dws = B.sb("dws", [128, KC, 32], F32)
    b_dw = Buf()
    P.dma("sp", dws[:].rearrange("p a b -> p (a b)"), dwv[:, :], writes=[b_dw])
    dwb, lng, lnb, b2 = C.v(V_X0), C.v(V_X1), C.v(V_X2), C.v(V_X3)
    NB_ = 256
    with contextlib.ExitStack() as st2:
        def sb2(name, shape, dt):
            return st2.enter_context(B.nc.sbuf_tensor(name, shape, dt))
        gin = sb2("gin", [128, KC, NB_ + 2 * HALO], F32)
        accA = sb2("accA", [128, KC, NB_], F32)
        accB = sb2("accB", [128, KC, NB_], F32)
        xb = sb2("xb", [128, KC, NB_], BF16)
        x2b = sb2("x2b", [128, KC, NB_], BF16)
        mean = sb2("mean", [128, NB_], F32)
        rstd = sb2("rstd", [128, NB_], F32)
        msq = sb2("msq", [128, NB_], F32)
        b_gin, b_A, b_B, b_xb, b_x2, b_mean, b_rstd = (Buf() for _ in range(7))
        segs = [(i * NB_, i * NB_, NB_) for i in range(NLAT // NB_)]
        segs.append((LATP, NLAT, NCTX))
        for (t0, nt) in [(0, TT), (TT, TT)]:
            sub = []
            for (a, b, n) in segs:
                lo, hi = max(b, t0), min(b + n, t0 + nt)
                if lo < hi:
                    sub.append((a + (lo - b), lo, hi - lo))
            for (a, tb, n) in sub:
                P.dma("sp", gin[:, :, :n + 2 * HALO], gp[:, a:a + n + 2 * HALO].rearrange("(c p) n -> p c n", p=128), writes=[b_gin])
                for c in range(KC):
                    P.op("dve", lambda h, c=c, n=n: h.tensor_scalar(out=accA[:, c, :n], in0=gin[:, c, 0:n], scalar1=dws[:, c, 0:1], scalar2=dwb[:, c:c + 1], op0=ALU.mult, op1=ALU.add),
                         reads=[b_gin, b_dw, C.b_vec], writes=[b_A])
                    for k in range(1, 16):
                        P.op("dve", lambda h, c=c, n=n, k=k: h.scalar_tensor_tensor(out=accA[:, c, :n], in0=gin[:, c, k:k + n], scalar=dws[:, c, k:k + 1], in1=accA[:, c, :n], op0=ALU.mult, op1=ALU.add),
                             reads=[b_gin, b_dw], writes=[b_A])
                    P.op("dve", lambda h, c=c, n=n: h.tensor_scalar(out=accB[:, c, :n], in0=gin[:, c, 16:16 + n], scalar1=dws[:, c, 16:17], scalar2=None, op0=ALU.mult),
                         reads=[b_gin, b_dw], writes=[b_B])
                    for k in range(17, 31):
                        P.op("dve", lambda h, c=c, n=n, k=k: h.scalar_tensor_tensor(out=accB[:, c, :n], in0=gin[:, c, k:k + n], scalar=dws[:, c, k:k + 1], in1=accB[:, c, :n], op0=ALU.mult, op1=ALU.add),
                             reads=[b_gin, b_dw], writes=[b_B])
                    P.op("dve", lambda h, c=c, n=n: h.tensor_tensor(out=accA[:, c, :n], in0=accA[:, c, :n], in1=accB[:, c, :n], op=ALU.add),
                         reads=[b_B], writes=[b_A])
                P.op("act", lambda h, n=n: h.activation(out=xb[:, :, :n], in_=accA[:, :, :n], func=AF.Copy), reads=[b_A], writes=[b_xb])
                P.op("act", lambda h, n=n: h.activation(out=x2b[:, :, :n], in_=accA[:, :, :n], func=AF.Square), reads=[b_A], writes=[b_x2])
                p1, p1b = B.next_ps()
                p2, p2b = B.next_ps()
                for kc in range(KC):
                    P.op("pe", lambda h, p1=p1, kc=kc, n=n: h.matmul(p1[:, :n], lhsT=B.ones_bf[:], rhs=xb[:, kc, :n], start=(kc == 0), stop=(kc == KC - 1)), reads=[b_xb, B.b_const], writes=[p1b])
                for kc in range(KC):
                    P.op("pe", lambda h, p2=p2, kc=kc, n=n: h.matmul(p2[:, :n], lhsT=B.ones_bf[:], rhs=x2b[:, kc, :n], start=(kc == 0), stop=(kc == KC - 1)), reads=[b_x2, B.b_const], writes=[p2b])
                P.op("dve", lambda h, p1=p1, n=n: h.tensor_scalar(out=mean[:, :n], in0=p1[:, :n], scalar1=1.0 / D, scalar2=None, op0=ALU.mult), reads=[p1b], writes=[b_mean])
                P.op("dve", lambda h, n=n: h.tensor_tensor(out=msq[:, :n], in0=mean[:, :n], in1=mean[:, :n], op=ALU.mult), reads=[b_mean], writes=[b_rstd])
                P.op("dve", lambda h, p2=p2, n=n: h.scalar_tensor_tensor(out=rstd[:, :n], in0=p2[:, :n], scalar=1.0 / D, in1=msq[:, :n], op0=ALU.mult, op1=ALU.subtract), reads=[p2b, b_rstd], writes=[b_rstd])
                P.op("act", lambda h, n=n: h.activation(out=rstd[:, :n], in_=rstd[:, :n], func=AF.Sqrt, bias=B.eps_c[:, 0:1]), reads=[b_rstd, B.b_const], writes=[b_rstd])
                P.op("dve", lambda h, n=n: h.reciprocal(out=rstd[:, :n], in_=rstd[:, :n]), reads=[b_rstd], writes=[b_rstd])
                o = tb - t0
                for c in range(KC):
                    P.op("dve", lambda h, c=c, n=n: h.tensor_tensor(out=accA[:, c, :n], in0=accA[:, c, :n], in1=mean[:, :n], op=ALU.subtract), reads=[b_mean], writes=[b_A])
                    P.op("dve", lambda h, c=c, n=n: h.tensor_tensor(out=accA[:, c, :n], in0=accA[:, c, :n], in1=rstd[:, :n], op=ALU.mult), reads=[b_rstd], writes=[b_A])
                    P.op("dve", lambda h, c=c, n=n: h.tensor_scalar(out=accA[:, c, :n], in0=accA[:, c, :n], scalar1=lng[:, c:c + 1], scalar2=lnb[:, c:c + 1], op0=ALU.mult, op1=ALU.add), reads=[C.b_vec], writes=[b_A])
                    P.op("act", lambda h, c=c, n=n, o=o: h.activation(out=C.uT[:, c, o:o + n], in_=accA[:, c, :n], func=AF.Silu), reads=[b_A], writes=[C.b_uT])
            emit_proj_residual(B, C.uT, C.b_uT, w2c, b2, C.v(V_G1L), C.v(V_G1C), C.b_vec, hT, hT_out, C.hrows, t0, nt, C.rmw)
    C.mlp(hT_out, w1, w2)
    return B.finish(C.hrows)


_PROGS = {}


def prog(name, fn, *a):
    key = (name,) + a
    if key not in _PROGS:
        _PROGS[key] = fn(*a)
    return _PROGS[key]


def launch(nc, in_maps):
    res = run_bass_kernel_spmd(nc, in_maps, core_ids=list(range(N_CORES)))
    return res.results


def host_mod(inp):
    c, c_ctx = inp["c"], inp["c_ctx"]
    cv = np.stack([c[0], c[1], c_ctx], 0).astype(np.float32)
    cT = np.ascontiguousarray(cv.reshape(3, KC, 128).transpose(2, 1, 0)).reshape(128, KC * 3)
    maps = []
    for r in range(N_CORES):
        layer, half = r // 2, r % 2
        mw = np.ascontiguousarray(inp["mod_w"][layer][:, half * 6144:(half + 1) * 6144])
        mb = np.ascontiguousarray(inp["mod_b"][layer, half * 6144:(half + 1) * 6144].reshape(48, 128).T)
        maps.append({"cT": cT, "mw": mw, "mb": mb})
    res = launch(prog("mod", build_mod), maps)
    mod = {}
    for r in range(N_CORES):
        layer, half = r // 2, r % 2
        m = np.asarray(res[r]["modT"]).reshape(128, 48, 3)
        for sl in range(3):
            for j in range(3):
                mod[(layer, half * 3 + sl, j)] = np.ascontiguousarray(m[:, sl * 16:(sl + 1) * 16, j])
    return mod


def layer_vec(inp, mod, i, b, extra=()):
    g = [None] * NVEC
    g[V_NMIX] = fvec(inp["norm_mix_g"][i]); g[V_NMLP] = fvec(inp["norm_mlp_g"][i])
    for (row, (sh1, s1, g1, sh2, s2, g2)) in ((b, (V_SH1L, V_S1L, V_G1L, V_SH2L, V_S2L, V_G2L)), (2, (V_SH1C, V_S1C, V_G1C, V_SH2C, V_S2C, V_G2C))):
        for slot, col in enumerate((sh1, s1, g1, sh2, s2, g2)):
            g[col] = mod[(i, slot, row)]
    for k, v in enumerate(extra):
        g[V_X0 + k] = fvec(v)
    z = np.zeros((128, 16), np.float32)
    return np.ascontiguousarray(np.concatenate([x if x is not None else z for x in g], 1))


def core_tokens(hl, hc, r):
    b, p = r // 4, r % 4
    return np.ascontiguousarray(np.concatenate([hl[b, p * NLAT:(p + 1) * NLAT], hc[b]], 0).T)


def host_conv_layer(inp, mod, i, hT_list, final=False):
    j = i // 3
    w1c = inp["conv_w1"][j]
    b1 = inp["conv_b1"][j]
    maps = []
    for r in range(N_CORES):
        vec = layer_vec(inp, mod, i, r // 4, (b1[:D], b1[D:]))
        maps.append({"hT": hT_list[r], "w1": w1c, "vec": vec})
    res = launch(prog("conv_a", build_conv_a), maps)
    glu = [np.asarray(res[r]["gluT"]) for r in range(N_CORES)]
    dwt = inp["conv_dw"][j]
    dw = np.zeros((128, KC, 32), np.float32)
    dw[:, :, :31] = dwt.T.reshape(KC, 128, 31).transpose(1, 0, 2)
    dw = dw.reshape(128, KC * 32)
    maps = []
    for r in range(N_CORES):
        b, p = r // 4, r % 4
        gp = np.zeros((D, LATP + CTXP), np.float32)
        gp[:, HALO:HALO + NLAT] = glu[r][:, :NLAT]
        if p > 0:
            gp[:, :HALO] = glu[r - 1][:, NLAT - HALO:NLAT]
        if p < 3:
            gp[:, HALO + NLAT:LATP] = glu[r + 1][:, :HALO]
        gp[:, LATP + HALO:LATP + HALO + NCTX] = glu[r][:, NLAT:]
        vec = layer_vec(inp, mod, i, b, (inp["conv_dwb"][j], inp["conv_ln_g"][j], inp["conv_ln_b"][j], inp["conv_b2"][j]))
        maps.append({"hT": hT_list[r], "gp": gp, "dw": dw, "cw2": inp["conv_w2"][j], "w1": inp["mlp_w1"][i], "w2": inp["mlp_w2"][i], "vec": vec})
    res = launch(prog("conv_b", build_conv_b), maps)
    return [np.asarray(res[r]["hT_out"]) for r in range(N_CORES)]


def fvec(v):
    return np.ascontiguousarray(np.asarray(v, np.float32).reshape(16, 128).T)


def emit_headnorm_front(B, C, oin, gs, t0, nt, nheads, hd, gt, b_gt, ident, fr):
    P = B.P
    ot, otb, sq, sqb, ss, ssb, ob, obb, gsb_t, gsb = fr
    for sub in range(nt // 128):
        s = sub % 2
        r0 = t0 + sub * 128
        P.dma("sp", ot[s][:], oin[r0:r0 + 128, :], writes=[otb[s]])
        if gs is not None:
            P.dma("sp", gsb_t[s][:], gs[r0:r0 + 128, :], writes=[gsb[s]])
        P.op("act", lambda h, s=s: h.activation(out=sq[:], in_=ot[s][:], func=AF.Square), reads=[otb[s]], writes=[sqb])
        P.op("dve", lambda h: h.tensor_reduce(out=ss[:, :nheads], in_=sq[:].rearrange("p (a b) -> p a b", a=nheads), axis=AX.X, op=ALU.add), reads=[sqb], writes=[ssb])
        P.op("act", lambda h: h.activation(out=ss[:, :nheads], in_=ss[:, :nheads], func=AF.Sqrt, scale=1.0 / hd, bias=B.eps_c[:, 0:1]), reads=[ssb, B.b_const], writes=[ssb])
        P.op("dve", lambda h: h.reciprocal(out=ss[:, :nheads], in_=ss[:, :nheads]), reads=[ssb], writes=[ssb])
        for hh_ in range(nheads):
            if gs is None:
                P.op("dve", lambda h, s=s, hh_=hh_: h.scalar_tensor_tensor(out=ob[s][:, hh_ * hd:(hh_ + 1) * hd], in0=ot[s][:, hh_ * hd:(hh_ + 1) * hd], scalar=ss[:, hh_:hh_ + 1], in1=gt[:, :hd], op0=ALU.mult, op1=ALU.mult),
                     reads=[otb[s], ssb, b_gt], writes=[obb[s]])
            else:
                P.op("dve", lambda h, s=s, hh_=hh_: h.scalar_tensor_tensor(out=ot[s][:, hh_ * hd:(hh_ + 1) * hd], in0=ot[s][:, hh_ * hd:(hh_ + 1) * hd], scalar=ss[:, hh_:hh_ + 1], in1=gt[:, :hd], op0=ALU.mult, op1=ALU.mult),
                     reads=[otb[s], ssb, b_gt], writes=[otb[s]])
        if gs is not None:
            P.op("dve", lambda h, s=s: h.tensor_tensor(out=ob[s][:], in0=ot[s][:], in1=gsb_t[s][:], op=ALU.mult), reads=[otb[s], gsb[s]], writes=[obb[s]])
        for half in range(2):
            ps, psb = B.next_ps()
            pv = ps[:].bitcast(BF16)
            for c8 in range(8):
                c = half * 8 + c8
                P.op("pe", lambda h, pv=pv, s=s, c=c, c8=c8: h.transpose(out=pv[:, c8 * 128:(c8 + 1) * 128], in_=ob[s][:, c * 128:(c + 1) * 128], identity=ident[:]),
                     reads=[obb[s], B.b_const], writes=[psb])
            eng = "act" if half == 0 else "dve"
            if eng == "act":
                P.op("act", lambda h, pv=pv, half=half, sub=sub: h.activation(out=C.uT[:, half * 8:(half + 1) * 8, sub * 128:(sub + 1) * 128], in_=pv.rearrange("p (a b) -> p a b", a=8), func=AF.Copy),
                     reads=[psb], writes=[C.b_uT])
            else:
                P.op("dve", lambda h, pv=pv, half=half, sub=sub: h.tensor_copy(out=C.uT[:, half * 8:(half + 1) * 8, sub * 128:(sub + 1) * 128], in_=pv.rearrange("p (a b) -> p a b", a=8)),
                     reads=[psb], writes=[C.b_uT])


def build_stage_c(kind, lam_init=0.0, final=False):
    B = Builder()
    P = B.P
    nheads, hd = (8, 256) if kind == "attn" else (4, 512)
    hT = B.din("hT", [D, NTOK])
    oin = B.din("oin", [NTOK, D])
    gs = B.din("gs", [NTOK, D], BF16) if kind == "gla" else None
    gvec = B.din("gvec", [128, hd])
    identd = B.din("ident", [128, 128])
    wo = B.din("wo", [D, D])
    w1 = B.din("w1", [D, HID])
    w2 = B.din("w2", [HID, D])
    hT_out = B.dout("hT_out", [D, NTOK])
    outT = B.dout("outT", [D, NLAT]) if final else None
    C = Common(B, True)
    ident = B.sb("ident_s", [128, 128], BF16)
    gt = B.sb("gt", [128, hd], F32)
    b_gt = Buf()
    P.dma("pool", ident[:], identd[:, :], writes=[B.b_const])
    P.dma("sp", gt[:], gvec[:, :], writes=[b_gt])
    if kind == "attn":
        P.op("dve", lambda h: h.tensor_scalar(out=gt[:], in0=gt[:], scalar1=1.0 - lam_init, scalar2=None, op0=ALU.mult), reads=[b_gt], writes=[b_gt])
    with contextlib.ExitStack() as st2:
        def sb2(name, shape, dt):
            return st2.enter_context(B.nc.sbuf_tensor(name, shape, dt))
        ot = [sb2(f"ot{i}", [128, D], F32) for i in range(2)]
        sq = sb2("osq", [128, D], F32)
        ss = sb2("oss", [128, 8], F32)
        ob = [sb2(f"ob{i}", [128, D], BF16) for i in range(2)]
        gst = [sb2(f"gst{i}", [128, D], BF16) for i in range(2)] if kind == "gla" else [None, None]
        fr = (ot, [Buf(), Buf()], sq, Buf(), ss, Buf(), ob, [Buf(), Buf()], gst, [Buf(), Buf()])
        for (t0, nt) in [(0, TT), (TT, TT)]:
            emit_headnorm_front(B, C, oin, gs, t0, nt, nheads, hd, gt, b_gt, ident, fr)
            emit_proj_residual(B, C.uT, C.b_uT, wo, None, C.v(V_G1L), C.v(V_G1C), C.b_vec, hT, hT_out, C.hrows, t0, nt, C.rmw)
    C.mlp(hT_out, w1, w2)
    outs = list(C.hrows)
    if final:
        outs += emit_final_norm(C, hT_out, outT, C.v(V_X0))
    return B.finish(outs)


def build_attn_a():
    B = Builder()
    P = B.P
    hT = B.din("hT", [D, NTOK])
    wqkv = B.din("wqkv", [D, 3 * D])
    csd = B.din("cs", [128, 2 * NLAT])
    rTd = B.din("rT", [128, 128])
    qT = B.dout("qT", [D, NTOK], BF16)
    kT = B.dout("kT", [D, NTOK], BF16)
    vo = B.dout("v", [NTOK, D], BF16)
    C = Common(B, False)
    cs = B.sb("cs_s", [128, 2, NLAT], F32)
    rT = B.sb("rT_s", [128, 128], BF16)
    b_cs = Buf()
    P.dma("sp", cs[:].rearrange("p a b -> p (a b)"), csd[:, :], writes=[b_cs])
    P.dma("pool", rT[:], rTd[:, :], writes=[B.b_const])
    xb = [B.sb(f"xb{i}", [128, BLK], BF16) for i in range(2)]
    xbb = [Buf(), Buf()]
    t1 = B.sb("t1", [128, BLK], F32)
    t2 = B.sb("t2", [128, BLK], F32)
    b_t1, b_t2 = Buf(), Buf()
    stg = [B.sb(f"stg{i}", [128, TT], BF16) for i in range(2)]
    stgb = [Buf(), Buf()]
    vst = [B.sb(f"vst{i}", [128, 512], BF16) for i in range(2)]
    vstb = [Buf(), Buf()]
    b_q, b_k, b_v = Buf(), Buf(), Buf()
    it = 0
    for (t0, nt) in [(0, TT), (TT, TT)]:
        C.norm_mix(hT, t0, nt)
        blks = tok_blocks(0, nt)
        for nb in range(8):
            wt, wb = B.load_w(wview(wqkv, 0, D, nb * 512, 512))
            dst, b_dst = (qT, b_q) if nb < 4 else (kT, b_k)
            for m in range(4):
                hm = (nb % 4) * 4 + m
                s = hm % 2
                for (o, n) in blks:
                    ps, psb = B.next_ps()
                    for kc in range(KC):
                        P.op("pe", lambda h, ps=ps, wt=wt, kc=kc, m=m, o=o, n=n: h.matmul(ps[:, :n], lhsT=wt[:, kc, m * 128:(m + 1) * 128], rhs=C.uT[:, kc, o:o + n], start=(kc == 0), stop=(kc == KC - 1)),
                             reads=[wb, C.b_uT], writes=[psb])
                    t = t0 + o
                    nl = max(0, min(t + n, NLAT) - t)
                    if nl > 0:
                        x = it % 2
                        it += 1
                        P.op("act", lambda h, ps=ps, x=x, nl=nl: h.activation(out=xb[x][:, :nl], in_=ps[:, :nl], func=AF.Copy), reads=[psb], writes=[xbb[x]])
                        pr, prb = B.next_ps()
                        P.op("pe", lambda h, pr=pr, x=x, nl=nl: h.matmul(pr[:, :nl], lhsT=rT[:], rhs=xb[x][:, :nl], start=True, stop=True), reads=[xbb[x], B.b_const], writes=[prb])
                        P.op("dve", lambda h, ps=ps, nl=nl, t=t: h.tensor_tensor(out=t1[:, :nl], in0=ps[:, :nl], in1=cs[:, 0, t:t + nl], op=ALU.mult), reads=[psb, b_cs, xbb[x]], writes=[b_t1])
                        P.op("dve", lambda h, pr=pr, nl=nl, t=t: h.tensor_tensor(out=t2[:, :nl], in0=pr[:, :nl], in1=cs[:, 1, t:t + nl], op=ALU.mult), reads=[prb, b_cs], writes=[b_t2])
                        P.op("dve", lambda h, s=s, o=o, nl=nl: h.tensor_tensor(out=stg[s][:, o:o + nl], in0=t1[:, :nl], in1=t2[:, :nl], op=ALU.add), reads=[b_t1, b_t2], writes=[stgb[s]])
                    if nl < n:
                        P.op("act", lambda h, ps=ps, s=s, o=o, nl=nl, n=n: h.activation(out=stg[s][:, o + nl:o + n], in_=ps[:, nl:n], func=AF.Copy), reads=[psb], writes=[stgb[s]])
                P.dma("sp", dst[hm * 128:(hm + 1) * 128, t0:t0 + nt], stg[s][:, :nt], reads=[stgb[s]], writes=[b_dst])
        for nb in range(8, 12):
            wt, wb = B.load_w(wview(wqkv, 0, D, nb * 512, 512))
            for sub in range(nt // 128):
                ps, psb = B.next_ps()
                for kc in range(KC):
                    P.op("pe", lambda h, ps=ps, wt=wt, kc=kc, sub=sub: h.matmul(ps[:, :512], lhsT=C.uT[:, kc, sub * 128:(sub + 1) * 128], rhs=wt[:, kc, :], start=(kc == 0), stop=(kc == KC - 1)),
                         reads=[wb, C.b_uT], writes=[psb])
                s = sub % 2
                if s == 0:
                    P.op("act", lambda h, ps=ps, s=s: h.activation(out=vst[s][:], in_=ps[:, :512], func=AF.Copy), reads=[psb], writes=[vstb[s]])
                else:
                    P.op("dve", lambda h, ps=ps, s=s: h.tensor_copy(out=vst[s][:], in_=ps[:, :512]), reads=[psb], writes=[vstb[s]])
                r0 = t0 + sub * 128
                P.dma("sp", vo[r0:r0 + 128, (nb - 8) * 512:(nb - 7) * 512], vst[s][:], reads=[vstb[s]], writes=[b_v])
    return B.finish([b_q, b_k, b_v])


NKEY = 8192 + NCTX
NKB = NKEY // 128
ATT_SCALE = 128 ** -0.5


def build_attn_b(lam_init):
    B = Builder()
    P = B.P
    QT = B.din("QT", [512, NKEY], BF16)
    KT = B.din("KT", [512, NKEY], BF16)
    V = B.din("V", [NKEY, 512], BF16)
    lv = B.din("lv", [128, 4])
    O = B.dout("O", [NKEY, 512])
    kt = B.sb("kt", [128, 4, NKEY], BF16)
    vaug = B.sb("vaug", [128, NKB, 2, 257], BF16)
    b_kt, b_va = Buf(), Buf()
    for hm in range(4):
        P.dma("sp", kt[:, hm, :], KT[hm * 128:(hm + 1) * 128, :], writes=[b_kt])
    P.op("pool", lambda h: h.memset(vaug[:].rearrange("p a b c -> p (a b c)"), 1.0), writes=[b_va])
    for hh_ in range(2):
        P.dma("sp", vaug[:, :, hh_, 0:256], V[:, hh_ * 256:(hh_ + 1) * 256].rearrange("(k p) e -> p k e", p=128), writes=[b_va])
    ones_f = B.sb("ones_f", [128, 128], F32)
    P.op("pool", lambda h: h.memset(ones_f[:], 1.0), writes=[B.b_const])
    lvs = B.sb("lvs", [128, 4], F32)
    pr2 = B.sb("pr2", [128, 2], F32)
    lam = B.sb("lam", [128, 2], F32)
    b_lv, b_lam = Buf(), Buf()
    P.dma("sp", lvs[:], lv[:, :], writes=[b_lv])
    P.op("dve", lambda h: h.tensor_tensor(out=pr2[:, 0:1], in0=lvs[:, 0:1], in1=lvs[:, 1:2], op=ALU.mult), reads=[b_lv], writes=[b_lam])
    P.op("dve", lambda h: h.tensor_tensor(out=pr2[:, 1:2], in0=lvs[:, 2:3], in1=lvs[:, 3:4], op=ALU.mult), reads=[b_lv], writes=[b_lam])
    ps, psb = B.ps[4], B.psb[4]
    P.op("pe", lambda h, ps=ps: h.matmul(ps[:, :2], lhsT=ones_f[:], rhs=pr2[:], start=True, stop=True), reads=[b_lam, B.b_const], writes=[psb])
    P.op("act", lambda h, ps=ps: h.activation(out=lam[:], in_=ps[:, :2], func=AF.Exp), reads=[psb], writes=[b_lam])
    P.op("dve", lambda h: h.tensor_tensor(out=lam[:, 0:1], in0=lam[:, 1:2], in1=lam[:, 0:1], op=ALU.subtract), reads=[b_lam], writes=[b_lam])
    P.op("dve", lambda h: h.tensor_scalar(out=lam[:, 0:1], in0=lam[:, 0:1], scalar1=-lam_init, scalar2=None, op0=ALU.add), reads=[b_lam], writes=[b_lam])
    sqt = [B.sb(f"sqt{i}", [128, 512], BF16) for i in range(2)]
    sqtb = [Buf(), Buf()]
    col = B.sb("col", [128, 1], F32)
    mx = B.sb("mx", [128, 8], F32)
    b_col, b_mx = Buf(), Buf()
    P.op("pool", lambda h: h.memset(mx[:], 0.0), writes=[b_mx])
    qt = [B.sb(f"qt{i}", [128, 4, 512], BF16) for i in range(2)]
    qtb = [Buf(), Buf()]
    qblocks = [(i * 512, 512) for i in range(16)] + [(8192, NCTX)]
    kblocks = tok_blocks(0, NKEY, 512)
    srot = [4]

    def next_st():
        i = srot[0]
        srot[0] = 4 + (srot[0] - 3) % 4
        return B.ps[i], B.psb[i]

    it = 0
    for which in range(2):
        blocks = kblocks if which == 0 else qblocks
        for bi, (c0, n) in enumerate(blocks):
            if which == 1:
                qs_ = bi % 2
                for hm in range(4):
                    P.dma("sp", qt[qs_][:, hm, :n], QT[hm * 128:(hm + 1) * 128, c0:c0 + n], writes=[qtb[qs_]])
            for hm in range(4):
                s = it % 2
                it += 1
                src = kt[:, hm, c0:c0 + n] if which == 0 else qt[qs_][:, hm, :n]
                rb = b_kt if which == 0 else qtb[qs_]
                P.op("act", lambda h, s=s, src=src, n=n: h.activation(out=sqt[s][:, :n], in_=src, func=AF.Square), reads=[rb], writes=[sqtb[s]])
                ps, psb = next_st()
                P.op("pe", lambda h, ps=ps, s=s, n=n: h.matmul(ps[:, :n], lhsT=B.ones_bf[:], rhs=sqt[s][:, :n], start=True, stop=True), reads=[sqtb[s], B.b_const], writes=[psb])
                P.op("dve", lambda h, ps=ps, n=n: h.tensor_reduce(out=col[:], in_=ps[:, :n], axis=AX.X, op=ALU.max), reads=[psb], writes=[b_col])
                j = which * 4 + hm
                P.op("dve", lambda h, j=j: h.tensor_tensor(out=mx[:, j:j + 1], in0=mx[:, j:j + 1], in1=col[:], op=ALU.max), reads=[b_col], writes=[b_mx])
    negm = B.sb("negm", [128, 4], F32)
    P.op("dve", lambda h: h.tensor_tensor(out=negm[:], in0=mx[:, 0:4], in1=mx[:, 4:8], op=ALU.mult), reads=[b_mx], writes=[b_mx])
    P.op("act", lambda h: h.activation(out=negm[:], in_=negm[:], func=AF.Sqrt, scale=ATT_SCALE * ATT_SCALE), reads=[b_mx], writes=[b_mx])
    P.op("dve", lambda h: h.tensor_scalar(out=negm[:], in0=negm[:], scalar1=-1.0, scalar2=None, op0=ALU.mult), reads=[b_mx], writes=[b_mx])
    pt = [B.sb(f"pt{i}", [128, 512], BF16) for i in range(3)]
    ptb = [Buf() for _ in range(3)]
    o0 = B.sb("o0", [128, 4, 256], F32)
    b_o0 = Buf()
    obuf = [B.sb(f"obuf{i}", [128, 4, 512], F32) for i in range(2)]
    obb = [Buf(), Buf()]
    rz = B.sb("rz", [128, 2], F32)
    b_rz = Buf()
    b_O = Buf()
    pi = 0
    for qi, (q0, nq) in enumerate(qblocks):
        qs_ = qi % 2
        for hm in range(4):
            P.dma("sp", qt[qs_][:, hm, :nq], QT[hm * 128:(hm + 1) * 128, q0:q0 + nq], writes=[qtb[qs_]])
        nkb = NKB if q0 < 8192 else NCTX // 128
        nqs = nq // 128
        os_ = qi % 2
        for hh_ in range(2):
            for j in range(2):
                hm = hh_ * 2 + j
                for kb in range(nkb):
                    st, stb = next_st()
                    P.op("pe", lambda h, st=st, hm=hm, kb=kb, qs_=qs_, nq=nq: h.matmul(st[:, :nq], lhsT=kt[:, hm, kb * 128:(kb + 1) * 128], rhs=qt[qs_][:, hm, :nq], start=True, stop=True),
                         reads=[b_kt, qtb[qs_]], writes=[stb])
                    p = pi % 3
                    pi += 1
                    P.op("act", lambda h, st=st, p=p, nq=nq, hm=hm: h.activation(out=pt[p][:, :nq], in_=st[:, :nq], func=AF.Exp, scale=ATT_SCALE, bias=negm[:, hm:hm + 1]),
                         reads=[stb, b_mx], writes=[ptb[p]])
                    for qs in range(nqs):
                        P.op("pe", lambda h, qs=qs, p=p, kb=kb, hh_=hh_, nkb=nkb: h.matmul(B.ps[qs][:, :257], lhsT=pt[p][:, qs * 128:(qs + 1) * 128], rhs=vaug[:, kb, hh_, :], start=(kb == 0), stop=(kb == nkb - 1)),
                             reads=[ptb[p], b_va], writes=[B.psb[qs]])
                for qs in range(nqs):
                    acc, accb = B.ps[qs], B.psb[qs]
                    P.op("dve", lambda h, acc=acc, j=j: h.reciprocal(out=rz[:, j:j + 1], in_=acc[:, 256:257]), reads=[accb], writes=[b_rz])
                    if j == 0:
                        P.op("dve", lambda h, acc=acc, qs=qs: h.tensor_scalar(out=o0[:, qs, :], in0=acc[:, :256], scalar1=rz[:, 0:1], scalar2=None, op0=ALU.mult), reads=[accb, b_rz], writes=[b_o0])
                    else:
                        P.op("dve", lambda h: h.tensor_tensor(out=rz[:, 1:2], in0=rz[:, 1:2], in1=lam[:, 0:1], op=ALU.mult), reads=[b_rz, b_lam], writes=[b_rz])
                        P.op("dve", lambda h, acc=acc, qs=qs, hh_=hh_, os_=os_: h.scalar_tensor_tensor(out=obuf[os_][:, qs, hh_ * 256:(hh_ + 1) * 256], in0=acc[:, :256], scalar=rz[:, 1:2], in1=o0[:, qs, :], op0=ALU.mult, op1=ALU.add),
                             reads=[accb, b_rz, b_o0], writes=[obb[os_]])
        P.dma("sp", O[q0:q0 + nq, :].rearrange("(s p) e -> p s e", p=128), obuf[os_][:, :nqs, :], reads=[obb[os_]], writes=[b_O])
    return B.finish([b_O])


GDK = 256
GDV = 512


def build_gla_a():
    B = Builder()
    P = B.P
    hT = B.din("hT", [D, NTOK])
    win = B.din("win", [D, 6144])
    wa1 = B.din("wa1", [D, 32])
    wa2 = B.din("wa2", [32, 1024])
    bad = B.din("ba", [2, 1024])
    qT = B.dout("qT", [1024, NTOK], BF16)
    kT = B.dout("kT", [1024, NTOK], BF16)
    ko = B.dout("k", [NTOK, 1024], BF16)
    vo = B.dout("v", [NTOK, D], BF16)
    go = B.dout("gs", [NTOK, D], BF16)
    Lf = B.dout("Lf", [NTOK, 1024])
    Lb = B.dout("Lb", [NTOK, 1024])
    C = Common(B, False)
    wa1s = B.sb("wa1s", [128, KC, 32], BF16)
    wa2s = [B.sb(f"wa2s{i}", [16, 1024], BF16) for i in range(2)]
    bas = [B.sb(f"bas{i}", [1, 1024], BF16) for i in range(2)]
    one_c = B.sb("one_c", [128, 1], F32)
    b_g = Buf()
    P.op("pool", lambda h: h.memset(one_c[:], 1.0), writes=[B.b_const])
    P.dma("pool", wa1s[:], wa1.rearrange("(c p) n -> p c n", p=128), writes=[b_g])
    for i in range(2):
        P.dma("pool", wa2s[i][:], wa2[i * 16:(i + 1) * 16, :], writes=[b_g])
        P.dma("pool", bas[i][:], bad[i:i + 1, :], writes=[b_g])
    z1 = [B.sb(f"z1{i}", [16, TT], BF16) for i in range(2)]
    b_z1 = Buf()
    stg = [B.sb(f"stg{i}", [128, TT], BF16) for i in range(2)]
    stgb = [Buf(), Buf()]
    vst = [B.sb(f"vst{i}", [128, 512], BF16) for i in range(2)]
    vstb = [Buf(), Buf()]
    ex = B.sb("ex", [128, 512], F32)
    b_ex = Buf()
    lst = [B.sb(f"lst{i}", [128, 512], F32) for i in range(2)]
    lstb = [Buf(), Buf()]
    b_out = [Buf() for _ in range(7)]
    for (t0, nt) in [(0, TT), (TT, TT)]:
        C.norm_mix(hT, t0, nt)
        blks = tok_blocks(0, nt)
        nsub = nt // 128
        for nb in range(4):
            wt, wb = B.load_w(wview(win, 0, D, nb * 512, 512))
            dst, b_dst = (qT, b_out[0]) if nb < 2 else (kT, b_out[1])
            scl = GDK ** -0.5 if nb < 2 else 1.0
            for m in range(4):
                ch = (nb % 2) * 4 + m
                s = ch % 2
                for (o, n) in blks:
                    ps, psb = B.next_ps()
                    for kc in range(KC):
                        P.op("pe", lambda h, ps=ps, wt=wt, kc=kc, m=m, o=o, n=n: h.matmul(ps[:, :n], lhsT=wt[:, kc, m * 128:(m + 1) * 128], rhs=C.uT[:, kc, o:o + n], start=(kc == 0), stop=(kc == KC - 1)),
                             reads=[wb, C.b_uT], writes=[psb])
                    P.op("act", lambda h, ps=ps, s=s, o=o, n=n, scl=scl: h.activation(out=stg[s][:, o:o + n], in_=ps[:, :n], func=AF.Copy, scale=scl), reads=[psb], writes=[stgb[s]])
                P.dma("sp", dst[ch * 128:(ch + 1) * 128, t0:t0 + nt], stg[s][:, :nt], reads=[stgb[s]], writes=[b_dst])
            if nb >= 2:
                for sub in range(nsub):
                    ps, psb = B.next_ps()
                    for kc in range(KC):
                        P.op("pe", lambda h, ps=ps, wt=wt, kc=kc, sub=sub: h.matmul(ps[:, :512], lhsT=C.uT[:, kc, sub * 128:(sub + 1) * 128], rhs=wt[:, kc, :], start=(kc == 0), stop=(kc == KC - 1)),
                             reads=[wb, C.b_uT], writes=[psb])
                    s = sub % 2
                    P.op("dve", lambda h, ps=ps, s=s: h.tensor_copy(out=vst[s][:], in_=ps[:, :512]), reads=[psb], writes=[vstb[s]])
                    r0 = t0 + sub * 128
                    P.dma("sp", ko[r0:r0 + 128, (nb - 2) * 512:(nb - 1) * 512], vst[s][:], reads=[vstb[s]], writes=[b_out[2]])
        for nb in range(4, 12):
            wt, wb = B.load_w(wview(win, 0, D, nb * 512, 512))
            isg = nb >= 8
            dst, b_dst = (go, b_out[4]) if isg else (vo, b_out[3])
            for sub in range(nsub):
                ps, psb = B.next_ps()
                for kc in range(KC):
                    P.op("pe", lambda h, ps=ps, wt=wt, kc=kc, sub=sub: h.matmul(ps[:, :512], lhsT=C.uT[:, kc, sub * 128:(sub + 1) * 128], rhs=wt[:, kc, :], start=(kc == 0), stop=(kc == KC - 1)),
                         reads=[wb, C.b_uT], writes=[psb])
                s = sub % 2
                if isg:
                    P.op("act", lambda h, ps=ps, s=s: h.activation(out=vst[s][:], in_=ps[:, :512], func=AF.Silu), reads=[psb], writes=[vstb[s]])
                else:
                    P.op("dve", lambda h, ps=ps, s=s: h.tensor_copy(out=vst[s][:], in_=ps[:, :512]), reads=[psb], writes=[vstb[s]])
                r0 = t0 + sub * 128
                P.dma("sp", dst[r0:r0 + 128, (nb % 4) * 512:(nb % 4 + 1) * 512], vst[s][:], reads=[vstb[s]], writes=[b_dst])
        for d_ in range(2):
            for (o, n) in blks:
                ps, psb = B.next_ps()
                for kc in range(KC):
                    P.op("pe", lambda h, ps=ps, kc=kc, d_=d_, o=o, n=n: h.matmul(ps[:16, :n], lhsT=wa1s[:, kc, d_ * 16:(d_ + 1) * 16], rhs=C.uT[:, kc, o:o + n], start=(kc == 0), stop=(kc == KC - 1)),
                         reads=[b_g, C.b_uT], writes=[psb])
                P.op("act", lambda h, ps=ps, d_=d_, o=o, n=n: h.activation(out=z1[d_][:, o:o + n], in_=ps[:16, :n], func=AF.Copy), reads=[psb], writes=[b_z1])
            dstL, b_dst = (Lf, b_out[5]) if d_ == 0 else (Lb, b_out[6])
            for sub in range(nsub):
                for cb in range(2):
                    ps, psb = B.next_ps()
                    P.op("pe", lambda h, ps=ps, d_=d_, sub=sub, cb=cb: h.matmul(ps[:, :512], lhsT=z1[d_][:, sub * 128:(sub + 1) * 128], rhs=wa2s[d_][:, cb * 512:(cb + 1) * 512], start=True, stop=False),
                         reads=[b_z1, b_g], writes=[psb])
                    P.op("pe", lambda h, ps=ps, d_=d_, cb=cb: h.matmul(ps[:, :512], lhsT=B.ones_bf[0:1, :], rhs=bas[d_][:, cb * 512:(cb + 1) * 512], start=False, stop=True),
                         reads=[b_g, B.b_const], writes=[psb])
                    s = (sub * 2 + cb) % 2
                    P.op("act", lambda h, ps=ps: h.activation(out=ex[:], in_=ps[:, :512], func=AF.Exp, scale=-1.0), reads=[psb], writes=[b_ex])
                    P.op("act", lambda h, s=s: h.activation(out=lst[s][:], in_=ex[:], func=AF.Ln, bias=one_c[:, 0:1]), reads=[b_ex, B.b_const], writes=[lstb[s]])
                    r0 = t0 + sub * 128
                    P.dma("sp", dstL[r0:r0 + 128, cb * 512:(cb + 1) * 512], lst[s][:], reads=[lstb[s]], writes=[b_dst])
    return B.finish(b_out)


NSEQ = 8192 + NCTX
NTILE = NSEQ // 128


def build_gla_b():
    B = Builder()
    P = B.P
    qT = B.din("qT", [GDK, NSEQ], BF16)
    kT = B.din("kT", [GDK, NSEQ], BF16)
    kk = B.din("kk", [NSEQ, GDK], BF16)
    vv = B.din("vv", [NSEQ, GDV], BF16)
    L = [B.din("Lf", [NSEQ, GDK]), B.din("Lb", [NSEQ, GDK])]
    cmd = B.din("cm", [128, 6 * 128])
    O = B.dout("O", [NSEQ, GDV])
    cm = B.sb("cm_s", [128, 6, 128], F32)
    P.dma("sp", cm[:].rearrange("p a b -> p (a b)"), cmd[:, :], writes=[B.b_const])
    S32 = [[B.sb(f"S32_{d}{k}", [128, GDV], F32) for k in range(2)] for d in range(2)]
    Sbf = [[B.sb(f"Sbf_{d}{k}", [128, GDV], BF16) for k in range(2)] for d in range(2)]
    bS32 = [[Buf() for _ in range(2)] for _ in range(2)]
    bSbf = [[Buf() for _ in range(2)] for _ in range(2)]
    for d in range(2):
        for k in range(2):
            P.op("pool", lambda h, d=d, k=k: h.memset(S32[d][k][:], 0.0), writes=[bS32[d][k]])
            P.op("pool", lambda h, d=d, k=k: h.memset(Sbf[d][k][:], 0.0), writes=[bSbf[d][k]])

    def mk(name, shape, dt):
        return [[B.sb(f"{name}{d}{i}", shape, dt) for i in range(2)] for d in range(2)], [[Buf() for _ in range(2)] for _ in range(2)]

    qt, qtb = mk("gq", [128, 2, 128], BF16)
    ktt, ktb = mk("gk", [128, 2, 128], BF16)
    kkt, kkb = mk("gkk", [128, GDK], BF16)
    vvt, vvb = mk("gvv", [128, GDV], BF16)
    Lt, Ltb = mk("gL", [128, GDK], F32)
    ebp, ebpb = mk("ebp", [128, GDK], F32)
    ebn, ebnb = mk("ebn", [128, GDK], F32)
    qd, qdb = mk("qd", [128, 2, 128], BF16)
    ki, kib = mk("ki", [128, 2, 128], BF16)
    eD, eDb = mk("eD", [128, GDK], F32)
    ke, keb = mk("ke", [128, GDK], BF16)
    am, amb = mk("am", [128, 128], BF16)
    ot, otb = mk("got", [128, GDV], F32)
    ol, olb = mk("gol", [128, GDV], F32)
    order = [list(range(NTILE)), [1, 0] + list(range(NTILE - 1, 1, -1))]
    stepof = [{t: i for i, t in enumerate(order[d])} for d in range(2)]
    otile = [Buf(f"O{t}") for t in range(NTILE)]
    for step in range(NTILE):
        for d in range(2):
            T = order[d][step]
            s = step % 2
            c0 = T * 128
            first = stepof[d][T] < stepof[1 - d][T] or (stepof[d][T] == stepof[1 - d][T] and d == 0)
            P.dma("sp", qt[d][s][:], qT[:, c0:c0 + 128].rearrange("(c p) n -> p c n", p=128), writes=[qtb[d][s]])
            P.dma("sp", ktt[d][s][:], kT[:, c0:c0 + 128].rearrange("(c p) n -> p c n", p=128), writes=[ktb[d][s]])
            P.dma("sp", kkt[d][s][:], kk[c0:c0 + 128, :], writes=[kkb[d][s]])
            P.dma("sp", vvt[d][s][:], vv[c0:c0 + 128, :], writes=[vvb[d][s]])
            P.dma("sp", Lt[d][s][:], L[d][c0:c0 + 128, :], writes=[Ltb[d][s]])
            pb, pbb = B.next_ps()
            for kc in range(2):
                P.op("pe", lambda h, pb=pb, d=d, s=s, kc=kc: h.matmul(pb[:, kc * 128:(kc + 1) * 128], lhsT=Lt[d][s][:, kc * 128:(kc + 1) * 128], rhs=cm[:, d, :], start=True, stop=True),
                     reads=[Ltb[d][s], B.b_const], writes=[pbb])
            P.op("act", lambda h, pb=pb, d=d, s=s: h.activation(out=ebp[d][s][:], in_=pb[:, :GDK], func=AF.Exp), reads=[pbb], writes=[ebpb[d][s]])
            P.op("act", lambda h, pb=pb, d=d, s=s: h.activation(out=ebn[d][s][:], in_=pb[:, :GDK], func=AF.Exp, scale=-1.0), reads=[pbb], writes=[ebnb[d][s]])
            P.op("dve", lambda h, d=d, s=s: h.tensor_tensor(out=qd[d][s][:], in0=qt[d][s][:], in1=ebp[d][s][:].rearrange("p (a b) -> p a b", a=2), op=ALU.mult),
                 reads=[qtb[d][s], ebpb[d][s]], writes=[qdb[d][s]])
            P.op("dve", lambda h, d=d, s=s: h.tensor_tensor(out=ki[d][s][:], in0=ktt[d][s][:], in1=ebn[d][s][:].rearrange("p (a b) -> p a b", a=2), op=ALU.mult),
                 reads=[ktb[d][s], ebnb[d][s]], writes=[kib[d][s]])
            pD, pDb = B.next_ps()
            P.op("pe", lambda h, pD=pD, d=d, s=s: h.matmul(pD[:, :GDK], lhsT=cm[:, 2 + d, :], rhs=Lt[d][s][:], start=True, stop=True), reads=[Ltb[d][s], B.b_const], writes=[pDb])
            P.op("act", lambda h, pD=pD, d=d, s=s: h.activation(out=eD[d][s][:], in_=pD[:, :GDK], func=AF.Exp), reads=[pDb], writes=[eDb[d][s]])
            P.op("dve", lambda h, d=d, s=s: h.tensor_tensor(out=ke[d][s][:], in0=kkt[d][s][:], in1=eD[d][s][:], op=ALU.mult), reads=[kkb[d][s], eDb[d][s]], writes=[keb[d][s]])
            pA, pAb = B.next_ps()
            for kc in range(2):
                P.op("pe", lambda h, pA=pA, d=d, s=s, kc=kc: h.matmul(pA[:, :128], lhsT=ki[d][s][:, kc, :], rhs=qd[d][s][:, kc, :], start=(kc == 0), stop=(kc == 1)),
                     reads=[kib[d][s], qdb[d][s]], writes=[pAb])
            P.op("dve", lambda h, pA=pA, d=d, s=s: h.tensor_tensor(out=am[d][s][:], in0=pA[:, :128], in1=cm[:, 4 + d, :], op=ALU.mult), reads=[pAb, B.b_const], writes=[amb[d][s]])
            pO, pOb = B.next_ps()
            P.op("pe", lambda h, pO=pO, d=d, s=s: h.matmul(pO[:, :GDV], lhsT=am[d][s][:], rhs=vvt[d][s][:], start=True, stop=False), reads=[amb[d][s], vvb[d][s]], writes=[pOb])
            for kc in range(2):
                P.op("pe", lambda h, pO=pO, d=d, s=s, kc=kc: h.matmul(pO[:, :GDV], lhsT=qd[d][s][:, kc, :], rhs=Sbf[d][kc][:], start=False, stop=(kc == 1)),
                     reads=[qdb[d][s], bSbf[d][kc]], writes=[pOb])
            if first:
                P.op("act", lambda h, pO=pO, d=d, s=s: h.activation(out=ot[d][s][:], in_=pO[:, :GDV], func=AF.Copy), reads=[pOb], writes=[otb[d][s]])
            else:
                P.dma("sp", ol[d][s][:], O[c0:c0 + 128, :], reads=[otile[T]], writes=[olb[d][s]])
                P.op("dve", lambda h, pO=pO, d=d, s=s: h.tensor_tensor(out=ot[d][s][:], in0=pO[:, :GDV], in1=ol[d][s][:], op=ALU.add), reads=[pOb, olb[d][s]], writes=[otb[d][s]])
            P.dma("sp", O[c0:c0 + 128, :], ot[d][s][:], reads=[otb[d][s]], writes=[otile[T]])
            last = 127 if d == 0 else 0
            for kc in range(2):
                pS, pSb = B.next_ps()
                P.op("pe", lambda h, pS=pS, d=d, s=s, kc=kc: h.matmul(pS[:, :GDV], lhsT=ke[d][s][:, kc * 128:(kc + 1) * 128], rhs=vvt[d][s][:], start=True, stop=True),
                     reads=[keb[d][s], vvb[d][s]], writes=[pSb])
                P.op("dve", lambda h, pS=pS, d=d, s=s, kc=kc, last=last: h.scalar_tensor_tensor(out=S32[d][kc][:], in0=S32[d][kc][:], scalar=ebp[d][s][:, kc * 128 + last:kc * 128 + last + 1], in1=pS[:, :GDV], op0=ALU.mult, op1=ALU.add),
                     reads=[pSb, ebpb[d][s], bS32[d][kc]], writes=[bS32[d][kc]])
                P.op("act", lambda h, d=d, kc=kc: h.activation(out=Sbf[d][kc][:], in_=S32[d][kc][:], func=AF.Copy), reads=[bS32[d][kc]], writes=[bSbf[d][kc]])
    return B.finish(otile)


def rope_tables_T(p):
    t = np.arange(p * NLAT, (p + 1) * NLAT)
    inv = (10000.0 ** (-np.arange(0, 64, 2, dtype=np.float32) / np.float32(64))).astype(np.float32)
    ar = (t // 64).astype(np.float32)[:, None] * inv
    ac = (t % 64).astype(np.float32)[:, None] * inv
    ang = np.concatenate([ar, ar, ac, ac], -1).astype(np.float32)
    return np.ascontiguousarray(np.concatenate([np.cos(ang).T, np.sin(ang).T], 1).astype(np.float32))


def rope_rT():
    r = np.zeros((128, 128), np.float32)
    for base in (0, 64):
        for i in range(32):
            r[base + i + 32, base + i] = -1.0
            r[base + i, base + i + 32] = 1.0
    return r


def gather_seq(outs, key, b, feature_major):
    cores = [np.asarray(outs[b * 4 + p][key]) for p in range(4)]
    if feature_major:
        return np.concatenate([cores[0][:, NLAT:]] + [c[:, :NLAT] for c in cores], 1)
    return np.concatenate([cores[0][NLAT:]] + [c[:NLAT] for c in cores], 0)


def scatter_tok(Ofull, b_cols_list, r):
    p = r % 4
    lat = np.concatenate([o[NCTX + p * NLAT:NCTX + (p + 1) * NLAT] for o in b_cols_list], 1)
    ctx = np.concatenate([o[:NCTX] for o in b_cols_list], 1)
    return np.ascontiguousarray(np.concatenate([lat, ctx], 0))


def host_attn_layer(inp, mod, i, hT_list, final=False):
    j = i // 3
    lam_init = 0.8 - 0.6 * math.exp(-0.3 * i)
    rT = rope_rT()
    maps = []
    for r in range(N_CORES):
        maps.append({"hT": hT_list[r], "wqkv": inp["diff_w_qkv"][j], "vec": layer_vec(inp, mod, i, r // 4), "cs": rope_tables_T(r % 4), "rT": rT})
    outs = launch(prog("attn_a", build_attn_a), maps)
    lv = np.ascontiguousarray(np.stack([inp["diff_lq1"][j], inp["diff_lk1"][j], inp["diff_lq2"][j], inp["diff_lk2"][j]], 1).astype(np.float32))
    maps = []
    for r in range(N_CORES):
        b, hp = r // 4, r % 4
        qf = gather_seq(outs, "qT", b, True)
        kf = gather_seq(outs, "kT", b, True)
        vf = gather_seq(outs, "v", b, False)
        rows = slice(hp * 512, (hp + 1) * 512)
        QT = np.ascontiguousarray(np.concatenate([qf[rows, NCTX:], qf[rows, :NCTX]], 1))
        maps.append({"QT": QT, "KT": np.ascontiguousarray(kf[rows]), "V": np.ascontiguousarray(vf[:, rows]), "lv": lv})
    res = launch(prog("attn_b", build_attn_b, lam_init), maps)
    ident = np.eye(128, dtype=np.float32)
    gvec = np.ascontiguousarray(np.broadcast_to(inp["diff_subln_g"][j][None, :], (128, 256)).astype(np.float32))
    maps = []
    for r in range(N_CORES):
        b, p = r // 4, r % 4
        os_ = [np.asarray(res[b * 4 + hp]["O"]) for hp in range(4)]
        lat = np.concatenate([o[p * NLAT:(p + 1) * NLAT] for o in os_], 1)
        ctx = np.concatenate([o[8192:] for o in os_], 1)
        oin = np.ascontiguousarray(np.concatenate([lat, ctx], 0))
        maps.append({"hT": hT_list[r], "oin": oin, "gvec": gvec, "ident": ident, "wo": inp["diff_w_o"][j], "w1": inp["mlp_w1"][i], "w2": inp["mlp_w2"][i],
                     "vec": layer_vec(inp, mod, i, b, (inp["final_g"],))})
    res = launch(prog("stage_c", build_stage_c, "attn", lam_init, final), maps)
    return res


def gla_consts():
    s = np.arange(128)[:, None]
    c = np.arange(128)[None, :]
    g = -1.0 / 16.0
    cm = np.stack([(s <= c) * g, (s >= c) * g, (s > c) * g, (s < c) * g, (s <= c) * 1.0, (s >= c) * 1.0], 1).astype(np.float32)
    return np.ascontiguousarray(cm.reshape(128, 6 * 128))


def host_gla_layer(inp, mod, i, hT_list, final=False):
    j = i // 3
    wa1 = np.ascontiguousarray(np.concatenate([inp["gla_wa1_f"][j], inp["gla_wa1_b"][j]], 1))
    wa2 = np.ascontiguousarray(np.concatenate([inp["gla_wa2_f"][j], inp["gla_wa2_b"][j]], 0))
    ba = np.ascontiguousarray(np.stack([inp["gla_ba_f"][j], inp["gla_ba_b"][j]], 0))
    maps = []
    for r in range(N_CORES):
        maps.append({"hT": hT_list[r], "win": inp["gla_w_in"][j], "wa1": wa1, "wa2": wa2, "ba": ba, "vec": layer_vec(inp, mod, i, r // 4)})
    outs = launch(prog("gla_a", build_gla_a), maps)
    cm = gla_consts()
    seq = {}
    for b in range(2):
        seq[b] = {"qT": gather_seq(outs, "qT", b, True), "kT": gather_seq(outs, "kT", b, True), "k": gather_seq(outs, "k", b, False),
                  "v": gather_seq(outs, "v", b, False), "Lf": gather_seq(outs, "Lf", b, False), "Lb": gather_seq(outs, "Lb", b, False)}
    maps = []
    for r in range(N_CORES):
        b, hd = r // 4, r % 4
        sq = seq[b]
        maps.append({"qT": np.ascontiguousarray(sq["qT"][hd * 256:(hd + 1) * 256]), "kT": np.ascontiguousarray(sq["kT"][hd * 256:(hd + 1) * 256]),
                     "kk": np.ascontiguousarray(sq["k"][:, hd * 256:(hd + 1) * 256]), "vv": np.ascontiguousarray(sq["v"][:, hd * 512:(hd + 1) * 512]),
                     "Lf": np.ascontiguousarray(sq["Lf"][:, hd * 256:(hd + 1) * 256]), "Lb": np.ascontiguousarray(sq["Lb"][:, hd * 256:(hd + 1) * 256]), "cm": cm})
    res = launch(prog("gla_b", build_gla_b), maps)
    ident = np.eye(128, dtype=np.float32)
    gvec = np.ascontiguousarray(np.broadcast_to(inp["gla_norm_g"][j][None, :], (128, 512)).astype(np.float32))
    maps = []
    for r in range(N_CORES):
        b, p = r // 4, r % 4
        os_ = [np.asarray(res[b * 4 + hd]["O"]) for hd in range(4)]
        oin = scatter_tok(None, os_, r)
        maps.append({"hT": hT_list[r], "oin": oin, "gs": np.asarray(outs[r]["gs"]), "gvec": gvec, "ident": ident, "wo": inp["gla_w_o"][j],
                     "w1": inp["mlp_w1"][i], "w2": inp["mlp_w2"][i], "vec": layer_vec(inp, mod, i, b, (inp["final_g"],))})
    res = launch(prog("stage_c", build_stage_c, "gla", 0.0, final), maps)
    return res


def kernel(**inp):
    inp = {k: np.asarray(v) for k, v in inp.items()}
    mod = host_mod(inp)
    hT = [core_tokens(inp["x"], inp["ctx"], r) for r in range(N_CORES)]
    res = host_gla_layer(inp, mod, 0, hT)
    hT = [np.asarray(res[r]["hT_out"]) for r in range(N_CORES)]
    hT = host_conv_layer(inp, mod, 1, hT)
    res = host_attn_layer(inp, mod, 2, hT)
    hT = [np.asarray(res[r]["hT_out"]) for r in range(N_CORES)]
    res = host_gla_layer(inp, mod, 3, hT, final=True)
    out = np.zeros((2, 8192, D), np.float32)
    for r in range(N_CORES):
        b, p = r // 4, r % 4
        out[b, p * NLAT:(p + 1) * NLAT] = np.asarray(res[r]["outT"]).T
    return out
```
